# Optimizing a Trainium2 kernel written in Bass

```python
import math
import jax, jax.numpy as jnp
from jax import lax
import numpy as np

D_MODEL = 2048
BATCH = 16
SEQ = 2048
DEPTH = 1
DEC_BATCH = 8
DEC_SEQ = 64
PAST_LEN = 2048

CHUNK = 64
MIX_W = D_MODEL
ATT_W = MIX_W // 2
LRU_W = MIX_W - ATT_W
N_HEADS = 8
V_DIM = ATT_W // N_HEADS
QK_DIM = V_DIM // 2
ROT_DIM = QK_DIM // 4
ROPE_THETA = 500000.0
Q_BLOCK = 128
N_LRU_BLOCKS = 8
LRU_BLOCK = LRU_W // N_LRU_BLOCKS
CONV_W = 4
LRU_C = 8.0
D_FF = 4 * D_MODEL
N_MOD = 6
IN_COLS = 3 * ATT_W + 2 * LRU_W
EPS = 1e-6

kernel_name = "chunk_causal_hymba_diffattn_rglru_step"


def rms_norm(x, eps=EPS):
    xf = x.astype(jnp.float32)
    return (xf * lax.rsqrt(jnp.mean(xf * xf, axis=-1, keepdims=True) + eps)).astype(x.dtype)


def lambda_init_fn(layer):
    return 0.8 - 0.6 * math.exp(-0.3 * layer)


def partial_rope(x, pos):
    half = ROT_DIM // 2
    inv_freq = ROPE_THETA ** (-(jnp.arange(half, dtype=jnp.float32) * 2.0) / ROT_DIM)
    ang = pos.astype(jnp.float32)[:, None] * inv_freq[None, :]
    cos = jnp.cos(ang)[:, None, None, :]
    sin = jnp.sin(ang)[:, None, None, :]
    xf = x.astype(jnp.float32)
    x1 = xf[..., :half]
    x2 = xf[..., half:ROT_DIM]
    out = jnp.concatenate([x1 * cos - x2 * sin, x2 * cos + x1 * sin, xf[..., ROT_DIM:]], axis=-1)
    return out.astype(x.dtype)


def modulation(c, w_ada, b_ada):
    mod = jax.nn.silu(c) @ w_ada + b_ada
    return jnp.split(mod[:, None, :], N_MOD, axis=-1)


def diff_attend(q, k, v, q_pos, k_pos, lam, g_subln, lam_init):
    s = jnp.einsum('bqhcd,bkhcd->bhcqk', q, k, preferred_element_type=jnp.float32) * (QK_DIM ** -0.5)
    visible = (k_pos[None, :] // CHUNK) <= (q_pos[:, None] // CHUNK)
    s = jnp.where(visible, s, -1e30)
    p = jax.nn.softmax(s, axis=-1)
    w = (p[:, :, 0] - lam * p[:, :, 1]).astype(v.dtype)
    o = jnp.einsum('bhqk,bkhd->bqhd', w, v)
    return rms_norm(o) * g_subln * (1.0 - lam_init)


def _lin_combine(left, right):
    a_l, b_l = left
    a_r, b_r = right
    return a_l * a_r, a_r * b_l + b_r


def rglru_branch(xb, gb, h0, conv0, conv_w, conv_b, w_a, b_a, w_x, b_x, lru_lambda):
    B, T, W = xb.shape
    buf = jnp.concatenate([conv0.astype(xb.dtype), xb], axis=1)
    u = conv_b + buf[:, 0:T] * conv_w[0]
    for j in range(1, CONV_W):
        u = u + buf[:, j:j + T] * conv_w[j]
    new_conv = buf[:, -(CONV_W - 1):]
    ub = u.reshape(B, T, N_LRU_BLOCKS, LRU_BLOCK)
    r = jax.nn.sigmoid(jnp.einsum('btnd,nde->btne', ub, w_a) + b_a).reshape(B, T, W)
    i = jax.nn.sigmoid(jnp.einsum('btnd,nde->btne', ub, w_x) + b_x).reshape(B, T, W)
    log_a = -LRU_C * r.astype(jnp.float32) * jax.nn.softplus(-lru_lambda.astype(jnp.float32))
    a = jnp.exp(log_a)
    b = jnp.sqrt(-jnp.expm1(2.0 * log_a)) * (i * u).astype(jnp.float32)
    b = b.at[:, 0].add(a[:, 0] * h0.astype(jnp.float32))
    _, h = lax.associative_scan(_lin_combine, (a, b), axis=1)
    y = h.astype(xb.dtype) * jax.nn.gelu(gb, approximate=True)
    return y, h[:, -1].astype(xb.dtype), new_conv


def hybrid_layer(x, c, pos, past_k, past_v, h0, conv0,
                 w_ada, b_ada, w_in, g_q, g_k, lambda_q1, lambda_k1, lambda_q2, lambda_k2, g_subln,
                 conv_w, conv_b, w_gate_a, b_gate_a, w_gate_x, b_gate_x, lru_lambda,
                 w_out, w_up, w_down, lam_init):
    B, T, _ = x.shape
    sh1, sc1, gt1, sh2, sc2, gt2 = modulation(c, w_ada, b_ada)
    xn = rms_norm(x) * (1.0 + sc1) + sh1
    proj = xn @ w_in
    q, k, v, xb, gb = jnp.split(proj, [ATT_W, 2 * ATT_W, 3 * ATT_W, 3 * ATT_W + LRU_W], axis=-1)
    q = partial_rope(rms_norm(q.reshape(B, T, N_HEADS, 2, QK_DIM)) * g_q, pos)
    k = partial_rope(rms_norm(k.reshape(B, T, N_HEADS, 2, QK_DIM)) * g_k, pos)
    v = v.reshape(B, T, N_HEADS, V_DIM)
    lam = (jnp.exp(jnp.sum(lambda_q1 * lambda_k1).astype(jnp.float32))
           - jnp.exp(jnp.sum(lambda_q2 * lambda_k2).astype(jnp.float32)) + lam_init)
    if past_k is None:
        nb = T // Q_BLOCK
        qb = jnp.moveaxis(q.reshape(B, nb, Q_BLOCK, N_HEADS, 2, QK_DIM), 1, 0)
        pb = pos.reshape(nb, Q_BLOCK)
        ob = lax.map(lambda a: diff_attend(a[0], k, v, a[1], pos, lam, g_subln, lam_init), (qb, pb))
        o = jnp.moveaxis(ob, 0, 1).reshape(B, T, ATT_W)
    else:
        k_all = jnp.concatenate([past_k.astype(k.dtype), k], axis=1)
        v_all = jnp.concatenate([past_v.astype(v.dtype), v], axis=1)
        k_pos = jnp.arange(k_all.shape[1])
        o = diff_attend(q, k_all, v_all, pos, k_pos, lam, g_subln, lam_init).reshape(B, T, ATT_W)
    y_lru, h_new, conv_new = rglru_branch(xb, gb, h0, conv0, conv_w, conv_b,
                                          w_gate_a, b_gate_a, w_gate_x, b_gate_x, lru_lambda)
    mix = jnp.concatenate([o, y_lru], axis=-1) @ w_out
    x = x + gt1 * mix
    xn2 = rms_norm(x) * (1.0 + sc2) + sh2
    x = x + gt2 * (jnp.square(jax.nn.relu(xn2 @ w_up)) @ w_down)
    return x, k, v, h_new, conv_new


def setup_inputs(seed: int = 0) -> dict:
    key = jax.random.key(seed)
    ks = jax.random.split(key, 32)
    L = DEPTH

    def nrm(k, shape, s):
        return jax.random.normal(k, shape, jnp.float32) * s

    a0 = jax.random.uniform(ks[20], (L, LRU_W), jnp.float32, minval=0.9, maxval=0.999)
    sg = a0 ** (1.0 / LRU_C)
    lru_lambda = jnp.log(sg) - jnp.log1p(-sg)
    return {
        "x_prompt": nrm(ks[0], (BATCH, SEQ, D_MODEL), 1.0),
        "x_sample": nrm(ks[1], (DEC_BATCH, DEC_SEQ, D_MODEL), 1.0),
        "c_prompt": nrm(ks[2], (BATCH, D_MODEL), 1.0),
        "c_sample": nrm(ks[3], (DEC_BATCH, D_MODEL), 1.0),
        "cache_k": nrm(ks[4], (L, DEC_BATCH, PAST_LEN, N_HEADS, 2, QK_DIM), 1.0),
        "cache_v": nrm(ks[5], (L, DEC_BATCH, PAST_LEN, N_HEADS, V_DIM), 1.0),
        "state_lru_h": nrm(ks[6], (L, DEC_BATCH, LRU_W), 0.5),
        "state_conv": nrm(ks[7], (L, DEC_BATCH, CONV_W - 1, LRU_W), 1.0),
        "w_ada": nrm(ks[8], (L, D_MODEL, N_MOD * D_MODEL), 0.5 * D_MODEL ** -0.5),
        "b_ada": nrm(ks[9], (L, N_MOD * D_MODEL), 0.02),
        "w_in": nrm(ks[10], (L, D_MODEL, IN_COLS), D_MODEL ** -0.5),
        "g_q": 1.0 + nrm(ks[11], (L, QK_DIM), 0.02),
        "g_k": 1.0 + nrm(ks[12], (L, QK_DIM), 0.02),
        "lambda_q1": nrm(ks[13], (L, QK_DIM), 0.1),
        "lambda_k1": nrm(ks[14], (L, QK_DIM), 0.1),
        "lambda_q2": nrm(ks[15], (L, QK_DIM), 0.1),
        "lambda_k2": nrm(ks[16], (L, QK_DIM), 0.1),
        "g_subln": 1.0 + nrm(ks[17], (L, V_DIM), 0.02),
        "conv_w": nrm(ks[18], (L, CONV_W, LRU_W), CONV_W ** -0.5),
        "conv_b": nrm(ks[19], (L, LRU_W), 0.02),
        "w_gate_a": nrm(ks[21], (L, N_LRU_BLOCKS, LRU_BLOCK, LRU_BLOCK), LRU_BLOCK ** -0.5),
        "b_gate_a": nrm(ks[22], (L, N_LRU_BLOCKS, LRU_BLOCK), 0.02),
        "w_gate_x": nrm(ks[23], (L, N_LRU_BLOCKS, LRU_BLOCK, LRU_BLOCK), LRU_BLOCK ** -0.5),
        "b_gate_x": nrm(ks[24], (L, N_LRU_BLOCKS, LRU_BLOCK), 0.02),
        "lru_lambda": lru_lambda,
        "w_out": nrm(ks[25], (L, MIX_W, D_MODEL), MIX_W ** -0.5),
        "w_up": nrm(ks[26], (L, D_MODEL, D_FF), D_MODEL ** -0.5),
        "w_down": nrm(ks[27], (L, D_FF, D_MODEL), D_FF ** -0.5),
    }


def reference(x_prompt, x_sample, c_prompt, c_sample, cache_k, cache_v, state_lru_h, state_conv,
              w_ada, b_ada, w_in, g_q, g_k, lambda_q1, lambda_k1, lambda_q2, lambda_k2, g_subln,
              conv_w, conv_b, w_gate_a, b_gate_a, w_gate_x, b_gate_x, lru_lambda,
              w_out, w_up, w_down):
    B, T, _ = x_prompt.shape
    Bs, Ts, _ = x_sample.shape
    pos_p = jnp.arange(T)
    pos_s = cache_k.shape[2] + jnp.arange(Ts)
    xp, xs = x_prompt, x_sample
    kp_l, vp_l, hp_l, cp_l = [], [], [], []
    ks_l, vs_l, hs_l, cs_l = [], [], [], []
    for d in range(DEPTH):
        lw = (w_ada[d], b_ada[d], w_in[d], g_q[d], g_k[d], lambda_q1[d], lambda_k1[d],
              lambda_q2[d], lambda_k2[d], g_subln[d], conv_w[d], conv_b[d], w_gate_a[d], b_gate_a[d],
              w_gate_x[d], b_gate_x[d], lru_lambda[d], w_out[d], w_up[d], w_down[d])
        lam_init = lambda_init_fn(d)
        h0 = jnp.zeros((B, LRU_W), xp.dtype)
        c0 = jnp.zeros((B, CONV_W - 1, LRU_W), xp.dtype)
        xp, kp, vp, hp, cp = hybrid_layer(xp, c_prompt, pos_p, None, None, h0, c0, *lw, lam_init)
        xs, kn, vn, hn, cn = hybrid_layer(xs, c_sample, pos_s, cache_k[d], cache_v[d],
                                          state_lru_h[d], state_conv[d], *lw, lam_init)
        kp_l.append(kp); vp_l.append(vp); hp_l.append(hp); cp_l.append(cp)
        ks_l.append(kn); vs_l.append(vn); hs_l.append(hn); cs_l.append(cn)
    return (xp, xs,
            jnp.stack(kp_l), jnp.stack(vp_l), jnp.stack(hp_l), jnp.stack(cp_l),
            jnp.stack(ks_l), jnp.stack(vs_l), jnp.stack(hs_l), jnp.stack(cs_l))
```

```python
import numpy as np
import concourse.bass as bass
import concourse.mybir as mybir
from concourse.bass_utils import run_bass_kernel_spmd

F32 = mybir.dt.float32
BF16 = mybir.dt.bfloat16
AF = mybir.ActivationFunctionType
ALU = mybir.AluOpType
AX = mybir.AxisListType

D = 2048
T = 2048
NPS = 2
TS = 64
PAST = 2048
NH = 8
DFF = 8192
INC = 5120
EPS = 1e-6
LAM_INIT = 0.8 - 0.6 * 1.0
BLK = 512
NKT = 17
NSCR = 11
SCRW = 516


class Buf:
    __slots__ = ("w", "r", "n")

    def __init__(self, n):
        self.n = n
        self.w = [None] * n
        self.r = [dict() for _ in range(n)]


class Tl:
    def __init__(self, h, cells=1, buf=None, excl=False):
        self.h = h
        self.buf = buf if buf is not None else Buf(cells)
        self.excl = excl

    def __getitem__(self, idx):
        return self.h[idx]


def _cells(t, c):
    if c is None:
        return range(t.buf.n)
    if isinstance(c, int):
        return (c,)
    return c


class KB:
    NDMA = {"sp": 24, "pool": 40, "act": 24}

    def __init__(self, nc):
        self.nc = nc
        self.E = {"pe": nc.tensor, "act": nc.scalar, "dve": nc.vector, "pool": nc.gpsimd, "sp": nc.sync}
        self.sems = []
        self.esem = {}
        self.ecnt = {}
        for n in self.E:
            self.esem[n] = self._new_sem("e_" + n)
            self.ecnt[n] = 0
        self.waited = {n: {} for n in self.E}
        self.dsem = {q: [self._new_sem("d%s%d" % (q, i)) for i in range(n)] for q, n in self.NDMA.items()}
        self.dcnt = {q: [0] * n for q, n in self.NDMA.items()}
        self.drr = {q: 0 for q in self.NDMA}
        self.out_tokens = []
        self.ninst = {n: 0 for n in self.E}

    def _new_sem(self, name):
        self.sems.append(self.nc.alloc_semaphore(name))
        return len(self.sems) - 1

    def _wait(self, eng, s, v):
        if self.waited[eng].get(s, 0) >= v:
            return
        for n, es in self.esem.items():
            if es == s:
                assert v <= self.ecnt[n], "wait on a mark not yet emitted: %s waits %s >= %d (cur %d)" % (eng, n, v, self.ecnt[n])
        self.E[eng].wait_ge(self.sems[s], v)
        self.waited[eng][s] = v

    def _collect(self, eng, reads, writes):
        need = {}
        own = self.esem[eng]

        def add(s, v):
            if need.get(s, 0) < v:
                need[s] = v

        for (t, c) in reads:
            b = t.buf
            for i in _cells(t, c):
                tok = b.w[i]
                if tok is None:
                    continue
                if tok[0] == own and eng == "pe":
                    continue
                add(*tok)
            if t.excl:
                for i in _cells(t, c):
                    for s, v in b.r[i].items():
                        if s != own:
                            add(s, v)
        strict = eng in ("act", "dve", "pool")
        for (t, c) in writes:
            b = t.buf
            for i in _cells(t, c):
                tok = b.w[i]
                if tok is not None and (tok[0] != own or strict):
                    add(*tok)
                for s, v in b.r[i].items():
                    if s != own or strict:
                        add(s, v)
        for s, v in need.items():
            self._wait(eng, s, v)

    def _commit(self, tok, reads, writes):
        s, v = tok
        for (t, c) in reads:
            b = t.buf
            for i in _cells(t, c):
                if b.r[i].get(s, 0) < v:
                    b.r[i][s] = v
        for (t, c) in writes:
            b = t.buf
            for i in _cells(t, c):
                b.w[i] = tok
                b.r[i] = {}

    def op(self, eng, fn, reads, writes, mark=True, **kw):
        self._collect(eng, reads, writes)
        inst = fn(**kw)
        self.ninst[eng] += 1
        s = self.esem[eng]
        if mark:
            inst.then_inc(self.sems[s], 1)
            self.ecnt[eng] += 1
            tok = (s, self.ecnt[eng])
        else:
            tok = (s, self.ecnt[eng] + 1)
        self._commit(tok, reads, writes)
        return tok

    def dma(self, q, out, in_, reads, writes, is_output=False, **kw):
        self._collect(q, reads, writes)
        i = self.drr[q]
        self.drr[q] = (i + 1) % self.NDMA[q]
        s = self.dsem[q][i]
        prev = self.dcnt[q][i]
        if prev > 0:
            self._wait(q, s, prev)
        inst = self.E[q].dma_start(out=out, in_=in_, **kw)
        inst.then_inc(self.sems[s], 16)
        self.ninst[q] += 1
        self.dcnt[q][i] = prev + 16
        tok = (s, prev + 16)
        self._commit(tok, reads, writes)
        if is_output:
            self.out_tokens.append(tok)
        return tok

    def finish(self):
        for q, n in self.NDMA.items():
            for i in range(n):
                if self.dcnt[q][i] > 0:
                    self._wait("sp", self.dsem[q][i], self.dcnt[q][i])
        for n in ("pe", "act", "dve", "pool"):
            if self.ecnt[n] > 0:
                self._wait("sp", self.esem[n], self.ecnt[n])


class _Stop(Exception):
    pass


def build(stop=None):
    try:
        return _build(stop)
    except _Stop as e:
        nc, kb = e.args
        kb.finish()
        return nc, kb


def _build(stop=None):
    nc = bass.Bass("TRN2", target_bir_lowering=False)
    kb = KB(nc)

    kb.phase_log = []

    def chk(tag):
        kb.phase_log.append((tag, dict(kb.ninst)))
        if stop == tag:
            raise _Stop(nc, kb)

    def din(name, shape, dt=F32):
        return nc.dram_tensor(name, list(shape), dt, kind="ExternalInput").ap()

    def dout(name, shape):
        return nc.dram_tensor(name, list(shape), F32, kind="ExternalOutput").ap()

    xp = din("xp", [NPS, T, D])
    xsm = din("xs", [TS, D])
    cT_d = din("cT", [128, 16, 3])
    ck_d = din("ck", [PAST, 1024])
    cv_d = din("cv", [PAST, 1024])
    h0_d = din("h0", [128, 8])
    conv0_d = din("conv0", [128, 8, 3])
    w_ada = din("w_ada", [D, 6 * D])
    w_in = din("w_in", [D, INC])
    w_out = din("w_out", [D, D])
    w_up = din("w_up", [D, DFF])
    w_down = din("w_down", [DFF, D])
    badaT_d = din("badaT", [128, 96])
    gq_d = din("gq_b", [128, 256])
    gk_d = din("gk_b", [128, 256])
    gsub_d = din("gsub_b", [128, 128])
    lams_d = din("lams_b", [128, 4, 64])
    convw_d = din("convw", [128, 8, 4])
    convb_d = din("convb", [128, 8])
    wga_d = din("wga", [8, 128, 128])
    wgx_d = din("wgx", [8, 128, 128])
    bga_d = din("bga", [128, 8])
    bgx_d = din("bgx", [128, 8])
    lru_d = din("lam_lru", [128, 8])
    cos_d = din("cos_t", [128, NKT, 8])
    sin_d = din("sin_t", [128, NKT, 8])
    ident_d = din("ident", [128, 128])

    yp = dout("yp", [NPS, T, D])
    ysm = dout("ys", [TS, D])
    kp_o = dout("kp", [NPS, T, 1024])
    vp_o = dout("vp", [NPS, T, 1024])
    hp_o = dout("hp", [NPS, 128, 8])
    cp_o = dout("cp", [NPS, 128, 8, 3])
    ks_o = dout("ksn", [TS, 1024])
    vs_o = dout("vsn", [TS, 1024])
    hs_o = dout("hsn", [128, 8])
    cs_o = dout("csn", [128, 8, 3])

    def dscr(name, shape):
        t = Tl(nc.dram_tensor(name, list(shape), BF16, kind="Internal").ap(), cells=(shape[0] // 2048) * (shape[1] // 1024))
        t.ncb = shape[1] // 1024
        return t

    w_in_bf = dscr("w_in_bf", [D, INC])
    w_out_bf = dscr("w_out_bf", [D, D])
    w_up_bf = dscr("w_up_bf", [D, DFF])
    w_down_bf = dscr("w_down_bf", [DFF, D])

    def sb(name, shape, dt=F32, cells=1):
        return Tl(nc.alloc_sbuf_tensor("sb_" + name, list(shape), dt), cells)

    ring = [sb("ring%d" % i, [128, 16, 256], BF16) for i in range(4)]
    ring_i = [0]
    kT = sb("kT", [128, NH, NKT * 128], BF16, cells=NH * NKT)
    vaug = sb("vaug", [128, NKT, NH, 130], BF16, cells=NKT * NH)
    hT = sb("hT", [128, 16, BLK], BF16, cells=16 * 4)
    qT = sb("qT", [128, NH, BLK], BF16, cells=NH * 4)
    mixT = sb("mixT", [128, 16, BLK], BF16, cells=16 * 4)
    xT = sb("xT", [128, 16, BLK], F32, cells=16 * 4)
    modT = sb("modT", [128, 96, 3], F32)
    scr = [sb("scr%d" % i, [128, SCRW], F32, cells=2) for i in range(NSCR)]
    scr_i = [0]

    role_cnt = {}

    def role(name, idxs):
        i = role_cnt.get(name, 0)
        role_cnt[name] = i + 1
        return scr[idxs[i % len(idxs)]]

    def bfv(t, n):
        return t.h[:].bitcast(BF16)[:, 0:n]

    gq_b = sb("gq_b", [128, 256])
    gk_b = sb("gk_b", [128, 256])
    gsub_b = sb("gsub_b", [128, 128])
    lamv = sb("lamv", [128, 8])
    convw = sb("convw", [128, 8, 4])
    convb = sb("convb", [128, 8])
    wga = sb("wga", [128, 8, 128], BF16)
    wgx = sb("wgx", [128, 8, 128], BF16)
    bga = sb("bga", [128, 8])
    bgx = sb("bgx", [128, 8])
    lru = sb("lru", [128, 8])
    cneg = sb("cneg", [128, 8])
    cos_t = sb("cos_t", [128, NKT, 8])
    sin_t = sb("sin_t", [128, NKT, 8])
    ident = sb("ident", [128, 128])
    ident_bf = sb("ident_bf", [128, 128], BF16)
    ones_bf = sb("ones_bf", [128, 128], BF16)
    cT = sb("cT", [128, 16, 3])
    scT = sb("scT", [128, 16, 3], BF16)
    badaT = sb("badaT", [128, 96])
    hcar = sb("hcar", [128, 8])
    ccar = sb("ccar", [128, 8, 3])
    stat = sb("stat", [128, 16], cells=16)
    stat_i = [0]
    qstat = sb("qstat", [128, 64], cells=4)
    qs_i = [0]
    nsin_t = sb("nsin_t", [128, NKT, 8])
    neghalf = sb("neghalf", [128, 8])
    pts = [(Tl(scr[i].h[:].bitcast(BF16)[:, hh * 512:(hh + 1) * 512], buf=scr[i].buf), hh)
           for i in range(4) for hh in range(2)]
    pt_i = [0]

    xst = [Tl(mixT.h[:, 8 * i:8 * i + 8, :].rearrange("p a b -> p (a b)").bitcast(F32), buf=mixT.buf) for i in range(2)]
    xst_cells = [range(32 * i, 32 * i + 32) for i in range(2)]
    xn_cells = range(0, 16)

    ps = [Tl(nc.alloc_psum_tensor("ps%d" % i, [128, 512], F32), excl=True) for i in range(8)]
    ps_i = [0]
    ps_lim = [8]

    def nps():
        t = ps[ps_i[0] % ps_lim[0]]
        ps_i[0] += 1
        return t

    E = kb.E
    pe, act, dve, pool = E["pe"], E["act"], E["dve"], E["pool"]

    def A(t, c=None):
        return (t, c)

    def mm(out, lhsT, rhs, start, stop, reads, writes, mark=None):
        return kb.op("pe", pe.matmul, reads, writes, mark=(stop if mark is None else mark), out=out, lhsT=lhsT,
                     rhs=rhs, start=start, stop=stop)

    def tr(out, in_, idt, reads, writes, mark=True):
        return kb.op("pe", pe.transpose, reads, writes, mark=mark, out=out, in_=in_, identity=idt)

    def actf(out, in_, func, reads, writes, **kw):
        return kb.op("act", act.activation, reads, writes, out=out, in_=in_, func=func, **kw)

    def ts(eng, out, in0, s1, s2, op0, op1, reads, writes):
        if op1 is None:
            return kb.op(eng, E[eng].tensor_scalar, reads, writes, out=out, in0=in0, scalar1=s1,
                         scalar2=None, op0=op0)
        return kb.op(eng, E[eng].tensor_scalar, reads, writes, out=out, in0=in0, scalar1=s1,
                     scalar2=s2, op0=op0, op1=op1)

    def tt(eng, out, in0, in1, op, reads, writes):
        return kb.op(eng, E[eng].tensor_tensor, reads, writes, out=out, in0=in0, in1=in1, op=op)

    def stt(out, in0, scalar, in1, op0, op1, reads, writes):
        return kb.op("dve", dve.scalar_tensor_tensor, reads, writes, out=out, in0=in0, scalar=scalar,
                     in1=in1, op0=op0, op1=op1)

    def cp(eng, out, in_, reads, writes):
        if eng == "act":
            return kb.op("act", act.copy, reads, writes, out=out, in_=in_)
        return kb.op(eng, E[eng].tensor_copy, reads, writes, out=out, in_=in_)

    def load(t, src, q="sp", cells=None, reads=()):
        return kb.dma(q, t.h[:] if isinstance(t, Tl) else t, src, list(reads), [A(t, cells)])

    lams = Tl(scr[10].h[:, 0:256].rearrange("p (a b) -> p a b", a=4), buf=scr[10].buf)
    for t, d in ((gq_b, gq_d), (gk_b, gk_d), (gsub_b, gsub_d), (lams, lams_d), (convw, convw_d),
                 (convb, convb_d), (bga, bga_d), (bgx, bgx_d), (lru, lru_d), (cos_t, cos_d),
                 (sin_t, sin_d), (ident, ident_d), (cT, cT_d), (badaT, badaT_d)):
        load(t, d)
    kb.dma("pool", wga.h[:], wga_d.rearrange("n d e -> d n e"), [], [A(wga)])
    kb.dma("pool", wgx.h[:], wgx_d.rearrange("n d e -> d n e"), [], [A(wgx)])

    cp("act", ident_bf.h[:], ident.h[:], [A(ident)], [A(ident_bf)])
    kb.op("dve", dve.memset, [], [A(ones_bf)], ap=ones_bf.h[:], constant=1.0)
    kb.op("dve", dve.memset, [], [A(neghalf)], ap=neghalf.h[:], constant=-0.5)
    kb.op("dve", dve.memset, [], [A(vaug)], ap=vaug.h[:, :, :, 128:130], constant=1.0)
    ts("dve", nsin_t.h[:], sin_t.h[:], -1.0, None, ALU.mult, None, [A(sin_t)], [A(nsin_t)])

    def load_sample_cache():
        for j in range(16):
            cells = [j * NH + h for h in range(NH)]
            kb.dma("pool", vaug.h[:, j, :, 0:128],
                   cv_d[j * 128:(j + 1) * 128, :].rearrange("p (h d) -> p h d", h=NH),
                   [], [A(vaug, cells)])
        for j in range(16):
            kst_t = role("k_a", [0, 2])
            kst2_t = role("k_b", [1, 3])
            for hf, tl in ((0, kst_t), (1, kst2_t)):
                kb.dma("pool", bfv(tl, 512), ck_d[j * 128:(j + 1) * 128, hf * 512:(hf + 1) * 512], [], [A(tl)])
            p_ = nps()
            pb = p_.h[:].bitcast(BF16)
            for h in range(NH):
                tl = kst_t if h < 4 else kst2_t
                tr(pb[:, h * 128:(h + 1) * 128], bfv(tl, 512)[:, (h % 4) * 128:(h % 4 + 1) * 128], ident_bf.h[:],
                   [A(tl), A(ident_bf)], [A(p_)], mark=(h == NH - 1))
            cells = [h * NKT + j for h in range(NH)]
            cp("dve" if j % 2 else "act", kT.h[:, :, j * 128:(j + 1) * 128],
               pb[:, 0:1024].rearrange("p (a b) -> p a b", a=NH), [A(p_)], [A(kT, cells)])


    s0 = scr[9]
    tt("dve", s0.h[:, 0:64], lams.h[:, 0, :], lams.h[:, 1, :], ALU.mult, [A(lams)], [A(s0)])
    tt("dve", s0.h[:, 64:128], lams.h[:, 2, :], lams.h[:, 3, :], ALU.mult, [A(lams)], [A(s0)])
    kb.op("dve", dve.tensor_reduce, [A(s0)], [A(lamv)], out=lamv.h[:, 0:2],
          in_=s0.h[:, 0:128].rearrange("p (a b) -> p a b", a=2), axis=AX.X, op=ALU.add)
    actf(lamv.h[:, 2:4], lamv.h[:, 0:2], AF.Exp, [A(lamv)], [A(lamv)])
    tt("dve", lamv.h[:, 4:5], lamv.h[:, 2:3], lamv.h[:, 3:4], ALU.subtract, [A(lamv)], [A(lamv)])
    ts("dve", lamv.h[:, 5:6], lamv.h[:, 4:5], LAM_INIT, -1.0, ALU.add, ALU.mult, [A(lamv)], [A(lamv)])
    neglam = lamv.h[:, 5:6]
    ts("dve", gsub_b.h[:], gsub_b.h[:], 1.0 - LAM_INIT, None, ALU.mult, None, [A(gsub_b)], [A(gsub_b)])
    actf(cneg.h[:], lru.h[:], AF.Exp, [A(lru)], [A(cneg)], scale=-1.0)
    actf(cneg.h[:], cneg.h[:], AF.Ln, [A(cneg)], [A(cneg)], bias=1.0)
    ts("dve", cneg.h[:], cneg.h[:], -8.0, None, ALU.mult, None, [A(cneg)], [A(cneg)])
    actf(scT.h[:], cT.h[:], AF.Silu, [A(cT)], [A(scT)])

    chk('consts')
    ring_free = [0, 1, 2, 3]
    for i_, r_ in enumerate(ring):
        r_.slot = i_

    def ring_next():
        assert ring_free, "weight ring exhausted"
        return ring[ring_free.pop(0)]

    def wfree(t):
        assert t.slot not in ring_free
        ring_free.append(t.slot)

    def wload(src_tl, r0, c0, q="sp", src_ap=None, cell=None):
        t = ring_next()
        if src_ap is None:
            src = src_tl.h[r0:r0 + 2048, c0:c0 + 256].rearrange("(k p) n -> p k n", p=128)
            kb.dma(q, t.h[:], src, [A(src_tl, (r0 // 2048) * src_tl.ncb + c0 // 1024)], [A(t)])
        else:
            src = src_ap[r0:r0 + 2048, c0:c0 + 256].rearrange("(k p) n -> p k n", p=128)
            kb.dma(q, t.h[:], src, [], [A(t)])
        return t

    def mod_chunks(mps_t, cc0, cc1, base=0):
        for cc in range(cc0, cc1):
            wt = wload(None, 0, cc * 256, q="pool", src_ap=w_ada)
            for f in range(2):
                ci = cc * 2 + f
                for kc in range(16):
                    mm(mps_t.h[:, (ci - base) * 3:(ci - base) * 3 + 3], wt.h[:, kc, f * 128:(f + 1) * 128],
                       scT.h[:, kc, :], kc == 0, kc == 15, [A(wt), A(scT)], [A(mps_t)])
            wfree(wt)

    def mod_evac(mps_t, ci0, ci1, base=0):
        n = ci1 - ci0
        tt("dve", modT.h[:, ci0:ci1, :], mps_t.h[:, (ci0 - base) * 3:(ci0 - base) * 3 + n * 3].rearrange("p (c b) -> p c b", b=3),
           badaT.h[:, ci0:ci1].unsqueeze(2).broadcast_to([128, n, 3]), ALU.add, [A(mps_t), A(badaT)], [A(modT)])

    mps = nps()
    mod_chunks(mps, 0, 16)
    mod_evac(mps, 0, 32)
    ts("dve", modT.h[:, 16:32, :], modT.h[:, 16:32, :], 1.0, None, ALU.add, None, [A(modT)], [A(modT)])
    mps2 = ps[7]

    def mod(m, fc, b):
        return modT.h[:, m * 16 + fc, b:b + 1]

    chk('mods')

    def conv_blk(dst, src, rg, cb):
        kb.dma("pool", dst.h[rg * 2048:(rg + 1) * 2048, cb * 1024:(cb + 1) * 1024],
               src[rg * 2048:(rg + 1) * 2048, cb * 1024:(cb + 1) * 1024], [], [A(dst, rg * dst.ncb + cb)])

    def cv_in(cb):
        return lambda: conv_blk(w_in_bf, w_in, 0, cb)

    def cv_out(cb):
        return lambda: conv_blk(w_out_bf, w_out, 0, cb)

    def cv_up(g):
        return lambda: [conv_blk(w_up_bf, w_up, 0, g * 2 + cb) for cb in range(2)]

    def cv_down(g):
        return lambda: [conv_blk(w_down_bf, w_down, g, cb) for cb in range(2)]

    def mods2(cc0, cc1):
        return lambda: mod_chunks(mps2, cc0, cc1, 32)

    def mods2_fin():
        mod_evac(mps2, 32, 96, 32)
        ts("dve", modT.h[:, 64:80, :], modT.h[:, 64:80, :], 1.0, None, ALU.add, None, [A(modT)], [A(modT)])

    conv_blk(w_in_bf, w_in, 0, 0)
    conv_blk(w_in_bf, w_in, 0, 1)
    first_hooks = {
        ("Q", 0): [cv_in(2), mods2(16, 22)], ("Q", 1): [cv_in(3), mods2(22, 28)], ("Q", 2): [cv_in(4), mods2(28, 34)],
        ("Q", 3): [cv_out(0), mods2(34, 40)], ("Q", 4): [cv_out(1), mods2(40, 44)],
        ("Q", 5): [cv_up(0), mods2(44, 48), mods2_fin],
        ("R", 0): [cv_down(0)], ("A", 0): [cv_up(1)], ("A", 1): [cv_down(1)],
    }
    second_hooks = {("Q", 1): [cv_up(2)], ("Q", 3): [cv_down(2)], ("A", 0): [cv_up(3)], ("A", 1): [cv_down(3)]}

    chk('convert')
    def run_block(b, seq_kind, x_src, y_dst, k_dst, v_dst, tok0, ntok, kt0, first, last, h_dst, c_dst,
                  hooks=None, ps_cap=8, direct_groups=()):
        R = min(128, ntok)
        ntile = (ntok + 127) // 128
        NT = ntok
        hooks = hooks or {}
        allt = lambda fc: [fc * 4 + t_ for t_ in range(ntile)]

        def hook(tag, i):
            for fn in hooks.get((tag, i), ()):
                fn()

        def st_col():
            i = stat_i[0] % 16
            stat_i[0] += 1
            return stat.h[:, i:i + 1], i

        ps_lim[0] = ps_cap

        def l_stage1(t):
            xs_ = xst[t % 2]
            xc = xst_cells[t % 2]
            kb.dma("sp", xs_.h[0:R, :], x_src[t * 128:t * 128 + R, :], [], [A(xs_, xc)])
            xn = qT.h[0:R, 4 * (t % 2):4 * (t % 2) + 4, :].rearrange("p a b -> p (a b)")
            xnc = range(16 * (t % 2), 16 * (t % 2) + 16)
            ss, ssc = st_col()
            rs, rsc = st_col()
            kb.op("act", act.activation, [A(xs_, xc)], [A(qT, xnc), A(stat, ssc)], out=xn, in_=xs_.h[0:R, :],
                  func=AF.Square, accum_out=ss[0:R, :])
            actf(rs[0:R, :], ss[0:R, :], AF.Sqrt, [A(stat, ssc)], [A(stat, rsc)], scale=1.0 / D, bias=EPS)
            kb.op("dve", dve.reciprocal, [A(stat, rsc)], [A(stat, rsc)], out=rs[0:R, :], in_=rs[0:R, :])
            ts("dve", xn, xs_.h[0:R, :], rs[0:R, :], None, ALU.mult, None, [A(xs_, xc), A(stat, rsc)],
               [A(qT, xnc)])
            return xs_, xc, xn, xnc

        def l_stage2(t, xs_, xc, xn, xnc):
            for g in range(4):
                p_ = nps()
                for j in range(4):
                    fc = g * 4 + j
                    tr(p_.h[:, j * R:(j + 1) * R], xs_.h[0:R, fc * 128:(fc + 1) * 128], ident.h[0:R, 0:R],
                       [A(xs_, xc), A(ident)], [A(p_)], mark=(j == 3))
                cells = [fc * 4 + t for fc in range(g * 4, g * 4 + 4)]
                src = p_.h[:, 0:4 * R].rearrange("p (a b) -> p a b", a=4)
                dstv = xT.h[:, g * 4:g * 4 + 4, t * 128:t * 128 + R]
                if g % 2 == 0:
                    cp("act", dstv, src, [A(p_)], [A(xT, cells)])
                else:
                    cp("dve", dstv, src, [A(p_)], [A(xT, cells)])
            for g in range(2):
                p_ = nps()
                pb = p_.h[:].bitcast(BF16)
                for j in range(8):
                    fc = g * 8 + j
                    tr(pb[:, j * R:(j + 1) * R], xn[:, fc * 128:(fc + 1) * 128], ident_bf.h[0:R, 0:R],
                       [A(qT, xnc), A(ident_bf)], [A(p_)], mark=(j == 7))
                for j in range(8):
                    fc = g * 8 + j
                    if g == 0:
                        actf(hT.h[:, fc, t * 128:t * 128 + R], pb[:, j * R:(j + 1) * R], AF.Identity,
                             [A(p_), A(modT)], [A(hT, fc * 4 + t)], scale=mod(1, fc, b), bias=mod(0, fc, b))
                    else:
                        ts("dve", hT.h[:, fc, t * 128:t * 128 + R], pb[:, j * R:(j + 1) * R], mod(1, fc, b),
                           mod(0, fc, b), ALU.mult, ALU.add, [A(p_), A(modT)], [A(hT, fc * 4 + t)])

        l_ctx = {0: l_stage1(0)}
        for t in range(ntile):
            if t + 1 < ntile:
                l_ctx[t + 1] = l_stage1(t + 1)
            l_stage2(t, *l_ctx[t])

        chk('L')
        def q_s1(pp):
            wA = wload(w_in_bf, 0, (2 * pp) * 256)
            wB = wload(w_in_bf, 0, (2 * pp + 1) * 256)
            kind = pp // 2
            h0_ = (pp % 2) * 4
            ctxs = []
            for t in range(ntile):
                p_ = nps()
                for half, wt in ((0, wA), (1, wB)):
                    for kc in range(16):
                        mm(p_.h[0:R, half * 256:(half + 1) * 256], hT.h[:, kc, t * 128:t * 128 + R], wt.h[:, kc, :],
                           kc == 0, kc == 15, [A(hT, kc * 4 + t), A(wt)], [A(p_)])
                pv = p_.h[0:R, :]
                if kind == 2:
                    vst = role("q_vst", [0, 1, 2])
                    cp("act", vst.h[0:R, 0:512], pv, [A(p_)], [A(vst)])
                    kb.dma("act", v_dst[t * 128:t * 128 + R, h0_ * 128:(h0_ + 4) * 128], vst.h[0:R, 0:512],
                           [A(vst)], [], is_output=True)
                    cells = [(kt0 + t) * NH + h0_ + j for j in range(4)]
                    cp("dve", vaug.h[0:R, kt0 + t, h0_:h0_ + 4, 0:128],
                       pv.rearrange("p (a b) -> p a b", a=4), [A(p_)], [A(vaug, cells)])
                    continue
                sqa = role("q_vst", [0, 1, 2])
                sq = role("q_sq", [3, 4, 5, 9])
                qk = role("q_qk", [6, 7, 8, 10])
                si = qs_i[0] % 4
                qs_i[0] += 1
                ss = qstat.h[0:R, si * 16:si * 16 + 8]
                rs = qstat.h[0:R, si * 16 + 8:si * 16 + 16]
                actf(sqa.h[0:R, 0:512], pv, AF.Square, [A(p_)], [A(sqa)])
                kb.op("dve", dve.tensor_reduce, [A(sqa)], [A(qstat, si)], out=ss,
                      in_=sqa.h[0:R, 0:512].rearrange("p (a b) -> p a b", a=8), axis=AX.X, op=ALU.add)
                ts("pool", rs, ss, 1.0 / 64, EPS, ALU.mult, ALU.add, [A(qstat, si)], [A(qstat, si)])
                tt("pool", rs, rs, neghalf.h[0:R, 0:8], ALU.pow, [A(qstat, si), A(neghalf)], [A(qstat, si)])
                ctxs.append((t, p_, sq, qk, si, rs))
            wfree(wA)
            wfree(wB)
            return pp, kind, h0_, ctxs

        def q_s2(st):
            pp, kind, h0_, ctxs = st
            gb_ = gq_b if kind == 0 else gk_b
            for (t, p_, sq, qk, si, rs) in ctxs:
                pv = p_.h[0:R, :]
                q8 = qk.h[0:R, 0:512].rearrange("p (a b) -> p a b", a=8)
                tt("dve", q8, pv.rearrange("p (a b) -> p a b", a=8), rs.unsqueeze(2).broadcast_to([R, 8, 64]),
                   ALU.mult, [A(p_), A(qstat, si)], [A(qk)])
                q2 = qk.h[0:R, 0:512].rearrange("p (a b) -> p a b", a=2)
                tt("pool", q2, q2, gb_.h[0:R, :].unsqueeze(1).broadcast_to([R, 2, 256]), ALU.mult,
                   [A(qk), A(gb_)], [A(qk)])

        def q_s34(st):
            pp, kind, h0_, ctxs = st
            p2s = {}
            for (t, p_, sq, qk, si, rs) in ctxs:
                q8 = qk.h[0:R, 0:512].rearrange("p (a b) -> p a b", a=8)
                x16 = q8[:, :, 0:16]
                x1 = q8[:, :, 0:8]
                x2 = q8[:, :, 8:16]
                tile_idx = kt0 + t
                cos4 = cos_t.h[0:R, tile_idx, :].unsqueeze(1).unsqueeze(1).broadcast_to([R, 8, 2, 8])
                sinb = sin_t.h[0:R, tile_idx, :].unsqueeze(1).broadcast_to([R, 8, 8])
                nsinb = nsin_t.h[0:R, tile_idx, :].unsqueeze(1).broadcast_to([R, 8, 8])
                tm = sq.h[0:R, 256:384].rearrange("p (a b) -> p a b", a=8)
                tt("dve", tm[:, :, 0:8], x2, nsinb, ALU.mult, [A(qk), A(nsin_t)], [A(sq)])
                tt("dve", tm[:, :, 8:16], x1, sinb, ALU.mult, [A(qk), A(sin_t)], [A(sq)])
                x16v = x16.rearrange("p g (a b) -> p g a b", a=2)
                tt("dve", x16v, x16v, cos4, ALU.mult, [A(qk), A(cos_t)], [A(qk)])
                tt("dve", x16, x16, tm, ALU.add, [A(qk), A(sq)], [A(qk)])
                if kind == 1:
                    kb.dma("act", k_dst[t * 128:t * 128 + R, h0_ * 128:(h0_ + 4) * 128], qk.h[0:R, 0:512],
                           [A(qk)], [], is_output=True)
                qb = bfv(sq, 512)
                cp("act", qb[0:R, :], qk.h[0:R, 0:512], [A(qk)], [A(sq)])
                p2 = nps()
                p2s[t] = p2
                pb = p2.h[:].bitcast(BF16)
                for j in range(4):
                    tr(pb[:, j * R:(j + 1) * R], qb[0:R, j * 128:(j + 1) * 128], ident_bf.h[0:R, 0:R],
                       [A(sq), A(ident_bf)], [A(p2)], mark=(j == 3))
            for (t, p_, sq, qk, si, rs) in ctxs:
                p2 = p2s[t]
                pb = p2.h[:].bitcast(BF16)
                src = pb[:, 0:4 * R].rearrange("p (a b) -> p a b", a=4)
                if kind == 0:
                    cells = [(h0_ + j) * 4 + t for j in range(4)]
                    cp("dve", qT.h[:, h0_:h0_ + 4, t * 128:t * 128 + R], src, [A(p2)], [A(qT, cells)])
                else:
                    cells = [(h0_ + j) * NKT + kt0 + t for j in range(4)]
                    cp("dve", kT.h[:, h0_:h0_ + 4, (kt0 + t) * 128:(kt0 + t) * 128 + R], src, [A(p2)],
                       [A(kT, cells)])
            hook('Q', pp)

        if ps_cap == 8:
            q_prev = q_s1(0)
            q_s2(q_prev)
            for pp in range(1, 6):
                q_cur = q_s1(pp)
                q_s34(q_prev)
                q_s2(q_cur)
                q_prev = q_cur
            q_s34(q_prev)
        else:
            for pp in range(6):
                q_cur = q_s1(pp)
                q_s2(q_cur)
                q_s34(q_cur)

        chk('Q')
        rw = {}

        def r_weights(pr):
            if pr not in rw:
                rw[pr] = (wload(w_in_bf, 0, 3072 + pr * 256), wload(w_in_bf, 0, 4096 + pr * 256))
            return rw[pr]

        def r_stage_a(fc):
            wx, _ = r_weights(fc // 2)
            f = fc % 2
            p_ = nps()
            for kc in range(16):
                mm(p_.h[:, 0:NT], wx.h[:, kc, f * 128:(f + 1) * 128], hT.h[:, kc, 0:NT], kc == 0, kc == 15,
                   [A(hT, allt(kc)), A(wx)], [A(p_)])
            xe = role("r_xe", [0, 1])
            cp("act", xe.h[:, 3:3 + NT], p_.h[:, 0:NT], [A(p_)], [A(xe)])
            cp("pool", xe.h[:, 0:3], ccar.h[:, fc, :], [A(ccar)], [A(xe)])
            cp("pool", ccar.h[:, fc, :], xe.h[:, NT:NT + 3], [A(xe)], [A(ccar)])
            u = role("r_u", [2, 9])
            ts("dve", u.h[:, 0:NT], xe.h[:, 0:NT], convw.h[:, fc, 0:1], convb.h[:, fc:fc + 1], ALU.mult, ALU.add,
               [A(xe), A(convw), A(convb)], [A(u)])
            for j in range(1, 4):
                stt(u.h[:, 0:NT], xe.h[:, j:j + NT], convw.h[:, fc, j:j + 1], u.h[:, 0:NT], ALU.mult, ALU.add,
                    [A(xe), A(convw), A(u)], [A(u)])
            ub_t = role("r_ub", [3, 10])
            cp("pool", bfv(ub_t, NT), u.h[:, 0:NT], [A(u)], [A(ub_t)])
            if f == 1:
                wfree(wx)
            return u, ub_t

        def r_stage_b(fc, u, ub_t):
            _, wg = r_weights(fc // 2)
            f = fc % 2
            ub = bfv(ub_t, NT)
            pa = nps()
            mm(pa.h[:, 0:NT], wga.h[:, fc, :], ub, True, True, [A(wga), A(ub_t)], [A(pa)])
            pi = nps()
            mm(pi.h[:, 0:NT], wgx.h[:, fc, :], ub, True, True, [A(wgx), A(ub_t)], [A(pi)])
            pg = nps()
            for kc in range(16):
                mm(pg.h[:, 0:NT], wg.h[:, kc, f * 128:(f + 1) * 128], hT.h[:, kc, 0:NT], kc == 0, kc == 15,
                   [A(hT, allt(kc)), A(wg)], [A(pg)])
            ra = role("r_ra", [4])
            ig = role("r_ig", [5])
            actf(ra.h[:, 0:NT], pa.h[:, 0:NT], AF.Sigmoid, [A(pa), A(bga)], [A(ra)], bias=bga.h[:, fc:fc + 1])
            actf(ig.h[:, 0:NT], pi.h[:, 0:NT], AF.Sigmoid, [A(pi), A(bgx)], [A(ig)], bias=bgx.h[:, fc:fc + 1])
            actf(ra.h[:, 0:NT], ra.h[:, 0:NT], AF.Exp, [A(ra), A(cneg)], [A(ra)], scale=cneg.h[:, fc:fc + 1])
            a2 = role("r_a2", [6])
            tt("dve", a2.h[:, 0:NT], ra.h[:, 0:NT], ra.h[:, 0:NT], ALU.mult, [A(ra)], [A(a2)])
            gl = role("r_gl", [8])
            actf(gl.h[:, 0:NT], pg.h[:, 0:NT], AF.Gelu_apprx_tanh, [A(pg)], [A(gl)])
            actf(a2.h[:, 0:NT], a2.h[:, 0:NT], AF.Sqrt, [A(a2)], [A(a2)], scale=-1.0, bias=1.0)
            tt("dve", ig.h[:, 0:NT], ig.h[:, 0:NT], u.h[:, 0:NT], ALU.mult, [A(ig), A(u)], [A(ig)])
            tt("dve", ig.h[:, 0:NT], ig.h[:, 0:NT], a2.h[:, 0:NT], ALU.mult, [A(ig), A(a2)], [A(ig)])
            hsq = role("r_hs", [7])
            kb.op("dve", dve.tensor_tensor_scan, [A(ra), A(ig), A(hcar)], [A(hsq)], out=hsq.h[:, 0:NT],
                  data0=ra.h[:, 0:NT], data1=ig.h[:, 0:NT], initial=hcar.h[:, fc:fc + 1], op0=ALU.mult,
                  op1=ALU.add)
            cp("pool", hcar.h[:, fc:fc + 1], hsq.h[:, NT - 1:NT], [A(hsq)], [A(hcar)])
            cells = [(8 + fc) * 4 + t for t in range(ntile)]
            tt("pool", mixT.h[:, 8 + fc, 0:NT], hsq.h[:, 0:NT], gl.h[:, 0:NT], ALU.mult, [A(hsq), A(gl)],
               [A(mixT, cells)])
            if f == 1:
                wfree(wg)

        hook('R', 0)
        nxt = r_stage_a(0)
        for fc in range(8):
            cur = nxt
            if fc + 1 < 8:
                nxt = r_stage_a(fc + 1)
            r_stage_b(fc, *cur)
            hook('Rf', fc)
        if last:
            kb.dma("act", h_dst, hcar.h[:], [A(hcar)], [], is_output=True)
            kb.dma("act", c_dst, ccar.h[:], [A(ccar)], [], is_output=True)

        chk('R')
        ps_lim[0] = 4
        nkt = kt0 + ntile
        oacc = ps[4:8]
        units = [(h, c, j) for h in range(NH) for c in range(2) for j in range(nkt)]
        LOOK = 3
        hstate = {}
        deferred = []
        cur_idx = [0]

        qz = [hT.h[:, 2 * i:2 * i + 2, :] for i in range(2)]
        qz_cells = [range(8 * i, 8 * i + 8) for i in range(2)]
        for i in range(2):
            kb.op("pool", pool.memset, [], [A(hT, qz_cells[i])], ap=qz[i][64:128, 0, 0:NT], constant=0.0)
            kb.op("pool", pool.memset, [], [A(hT, qz_cells[i])], ap=qz[i][0:64, 1, 0:NT], constant=0.0)

        def load_qz(h):
            i = h % 2
            qc = [h * 4 + t for t in range(ntile)]
            cp("pool", qz[i][0:64, 0, 0:NT], qT.h[0:64, h, 0:NT], [A(qT, qc)], [A(hT, qz_cells[i])])
            cp("pool", qz[i][64:128, 1, 0:NT], qT.h[64:128, h, 0:NT], [A(qT, qc)], [A(hT, qz_cells[i])])

        def s_stage(h, c, j):
            kr = 128 if (seq_kind == "p" or j < 16) else TS
            q0 = (j - kt0) * 128 if (seq_kind == "p" and j >= kt0) else 0
            if c == 0 and j == 0:
                if h == 0:
                    load_qz(0)
                if h + 1 < NH:
                    load_qz(h + 1)
            sp_ = nps()
            mm(sp_.h[0:kr, q0:NT], kT.h[:, h, j * 128:j * 128 + kr], qz[h % 2][:, c, q0:NT], True, True,
               [A(kT, h * NKT + j), A(hT, qz_cells[h % 2])], [A(sp_)])
            pt_tl, pc = pts[pt_i[0] % 8]
            pt_i[0] += 1
            pt = pt_tl.h
            actf(pt[0:kr, q0:NT], sp_.h[0:kr, q0:NT], AF.Exp, [A(sp_)], [A(pt_tl, pc)], scale=0.125)
            if seq_kind == "p" and j >= kt0:
                kb.op("pool", pool.memset, [], [A(pt_tl, pc)], ap=pt[64:128, q0:q0 + 64], constant=0.0)
            return pt_tl, pc, kr, q0

        def evac_t(h, c, t):
            if h not in hstate:
                hstate[h] = (role("a_osb", [4, 5]), role("a_rsum", [6, 7]))
            osb, rsum = hstate[h]
            rc = rsum.h[0:R, (c * 4 + t):(c * 4 + t) + 1]
            kb.op("dve", dve.reciprocal, [A(oacc[t])], [A(rsum)], out=rc, in_=oacc[t].h[0:R, 128:129])
            ov = osb.h[0:R, t * 128:(t + 1) * 128]
            if c == 0:
                ts("dve", ov, oacc[t].h[0:R, 0:128], rc, None, ALU.mult, None, [A(oacc[t]), A(rsum)], [A(osb)])
            else:
                ts("dve", rc, rc, neglam[0:R, :], None, ALU.mult, None, [A(rsum), A(lamv)], [A(rsum)])
                stt(ov, oacc[t].h[0:R, 0:128], rc, ov, ALU.mult, ALU.add, [A(oacc[t]), A(rsum), A(osb)], [A(osb)])

        def head_part2(h, ob_t):
            ob = bfv(ob_t, 512)
            p2 = nps()
            pb = p2.h[:].bitcast(BF16)
            for t in range(ntile):
                tr(pb[:, t * R:(t + 1) * R], ob[0:R, t * 128:(t + 1) * 128], ident_bf.h[0:R, 0:R],
                   [A(ob_t), A(ident_bf)], [A(p2)], mark=True)
            cells = [h * 4 + t for t in range(ntile)]
            cp("dve", mixT.h[:, h, 0:NT], pb[:, 0:NT], [A(p2)], [A(mixT, cells)])

        def finish_head(h):
            osb, rsum = hstate[h]
            sqj = role("a_sqj", [8])
            ob_t = role("a_ob", [9, 10])
            ob = bfv(ob_t, 512)
            for t in range(ntile):
                ov = osb.h[0:R, t * 128:(t + 1) * 128]
                kb.op("dve", dve.scalar_tensor_tensor, [A(osb)], [A(sqj), A(rsum)], out=sqj.h[0:R, 0:128], in0=ov,
                      scalar=1.0, in1=ov, op0=ALU.mult, op1=ALU.mult, accum_out=rsum.h[0:R, 8 + t:9 + t])
            ts("pool", rsum.h[0:R, 12:12 + ntile], rsum.h[0:R, 8:8 + ntile], 1.0 / 128, EPS, ALU.mult, ALU.add,
               [A(rsum)], [A(rsum)])
            tt("pool", rsum.h[0:R, 12:12 + ntile], rsum.h[0:R, 12:12 + ntile], neghalf.h[0:R, 0:ntile], ALU.pow,
               [A(rsum), A(neghalf)], [A(rsum)])
            for t in range(ntile):
                ov = osb.h[0:R, t * 128:(t + 1) * 128]
                stt(ob[0:R, t * 128:(t + 1) * 128], ov, rsum.h[0:R, 12 + t:13 + t], gsub_b.h[0:R, :], ALU.mult,
                    ALU.mult, [A(osb), A(rsum), A(gsub_b)], [A(ob_t)])
            deferred.append((cur_idx[0] + 6, h, ob_t))

        def pv_stage(h, c, j, info):
            pt_tl, pc, kr, q0 = info
            pt = pt_tl.h
            for t in range(q0 // 128, ntile):
                jl = (kt0 + t) if seq_kind == "p" else (nkt - 1)
                mm(oacc[t].h[0:128, 0:130], pt[0:kr, t * 128:t * 128 + 128], vaug.h[0:kr, j, h, :],
                   j == 0, j == jl, [A(pt_tl, pc), A(vaug, j * NH + h)], [A(oacc[t])])
                if j == jl:
                    evac_t(h, c, t)
            if j == nkt - 1 and c == 1:
                finish_head(h)

        def run_deferred(force=False):
            while deferred and (force or deferred[0][0] <= cur_idx[0]):
                _, h, ob_t = deferred.pop(0)
                head_part2(h, ob_t)

        hook('A', 0)
        pend = []
        for idx, u_ in enumerate(units):
            cur_idx[0] = idx
            if idx == len(units) // 2:
                hook('A', 1)
            pend.append((u_, s_stage(*u_)))
            if len(pend) > LOOK:
                uu, inf = pend.pop(0)
                pv_stage(*uu, inf)
            run_deferred()
        while pend:
            uu, inf = pend.pop(0)
            pv_stage(*uu, inf)
        run_deferred(force=True)
        ps_lim[0] = 8
        hook('O', 0)

        chk('A')
        hook('N', 0)
        ps_lim[0] = 7
        pn = ps[7]
        pend_sq = []

        def ones_mm(fc, sq_t):
            mm(pn.h[:, 0:NT], ones_bf.h[:], bfv(sq_t, NT), fc == 0, fc == 15, [A(ones_bf), A(sq_t)], [A(pn)], mark=True)

        for cc in range(8):
            wt = wload(w_out_bf, 0, cc * 256)
            for f in range(2):
                fc = cc * 2 + f
                p_ = nps()
                for kc in range(16):
                    mm(p_.h[:, 0:NT], wt.h[:, kc, f * 128:(f + 1) * 128], mixT.h[:, kc, 0:NT], kc == 0, kc == 15,
                       [A(mixT, allt(kc)), A(wt)], [A(p_)])
                if pend_sq:
                    ones_mm(*pend_sq.pop(0))
                stt(xT.h[:, fc, 0:NT], p_.h[:, 0:NT], mod(2, fc, b), xT.h[:, fc, 0:NT], ALU.mult, ALU.add,
                    [A(p_), A(modT), A(xT, allt(fc))], [A(xT, allt(fc))])
                sq_t = role("n_sq", [0, 1])
                actf(bfv(sq_t, NT), xT.h[:, fc, 0:NT], AF.Square, [A(xT, allt(fc))], [A(sq_t)])
                pend_sq.append((fc, sq_t))
            wfree(wt)
        while pend_sq:
            ones_mm(*pend_sq.pop(0))

        chk('O')
        chk('N1')
        rstd = role("n_rstd", [2])
        actf(rstd.h[:, 0:NT], pn.h[:, 0:NT], AF.Sqrt, [A(pn)], [A(rstd)], scale=1.0 / D, bias=EPS)
        kb.op("dve", dve.reciprocal, [A(rstd)], [A(rstd)], out=rstd.h[:, 0:NT], in_=rstd.h[:, 0:NT])
        chk('N2')
        for fc in range(16):
            if fc == 1:
                chk('N3')
            tmp = role("n_tmp", [3, 4])
            tt("pool" if fc % 2 else "dve", tmp.h[:, 0:NT], xT.h[:, fc, 0:NT], rstd.h[:, 0:NT], ALU.mult,
               [A(xT, allt(fc)), A(rstd)], [A(tmp)])
            actf(hT.h[:, fc, 0:NT], tmp.h[:, 0:NT], AF.Identity, [A(tmp), A(modT)], [A(hT, allt(fc))],
                 scale=mod(4, fc, b), bias=mod(3, fc, b))

        ps_lim[0] = 8
        chk('N')
        uT = mixT
        for g in range(4):
            hook('UD', g)
            for cc in range(8):
                if g in direct_groups:
                    wt = wload(None, 0, g * 2048 + cc * 256, q="pool", src_ap=w_up)
                else:
                    wt = wload(w_up_bf, 0, g * 2048 + cc * 256)
                for f in range(2):
                    j = cc * 2 + f
                    p_ = nps()
                    for kc in range(16):
                        mm(p_.h[:, 0:NT], wt.h[:, kc, f * 128:(f + 1) * 128], hT.h[:, kc, 0:NT], kc == 0, kc == 15,
                           [A(hT, allt(kc)), A(wt)], [A(p_)])
                    rl = role("u_rl", [5, 6, 7])
                    actf(rl.h[:, 0:NT], p_.h[:, 0:NT], AF.Relu, [A(p_)], [A(rl)])
                    tt("dve", uT.h[:, j, 0:NT], p_.h[:, 0:NT], rl.h[:, 0:NT], ALU.mult, [A(p_), A(rl)],
                       [A(uT, allt(j))])
                wfree(wt)
            for cc in range(8):
                if g in direct_groups:
                    wt = wload(None, g * 2048, cc * 256, q="pool", src_ap=w_down)
                else:
                    wt = wload(w_down_bf, g * 2048, cc * 256)
                for f in range(2):
                    fc = cc * 2 + f
                    p_ = nps()
                    for kc in range(16):
                        mm(p_.h[:, 0:NT], wt.h[:, kc, f * 128:(f + 1) * 128], uT.h[:, kc, 0:NT], kc == 0, kc == 15,
                           [A(uT, allt(kc)), A(wt)], [A(p_)])
                    stt(xT.h[:, fc, 0:NT], p_.h[:, 0:NT], mod(5, fc, b), xT.h[:, fc, 0:NT], ALU.mult, ALU.add,
                        [A(p_), A(modT), A(xT, allt(fc))], [A(xT, allt(fc))])
                wfree(wt)

        chk('UD')
        chk('Ystart')
        for t in range(ntile):
            for g in range(4):
                p_ = nps()
                for j in range(4):
                    fc = g * 4 + j
                    tr(p_.h[0:R, j * 128:(j + 1) * 128], xT.h[:, fc, t * 128:t * 128 + R], ident.h[:],
                       [A(xT, fc * 4 + t), A(ident)], [A(p_)], mark=(j == 3))
                yt = role("y_st", [0, 1, 2, 3, 4, 8, 9, 10])
                cp("act" if g % 2 == 0 else "dve", yt.h[0:R, 0:512], p_.h[0:R, :], [A(p_)], [A(yt)])
                kb.dma("act", y_dst[t * 128:t * 128 + R, g * 512:(g + 1) * 512], yt.h[0:R, 0:512], [A(yt)], [],
                       is_output=True)

    for s_ in range(NPS):
        kb.op("dve", dve.memset, [], [A(hcar)], ap=hcar.h[:], constant=0.0)
        kb.op("dve", dve.memset, [], [A(ccar)], ap=ccar.h[:], constant=0.0)
        for blk in range(T // BLK):
            tok0 = blk * BLK
            hk = None
            cap = 8
            if s_ == 0 and blk == 0:
                hk = first_hooks
                cap = 7
            if s_ == 0 and blk == 1:
                hk = second_hooks
            if s_ == NPS - 1 and blk == T // BLK - 1:
                hk = {("UD", 1): [load_sample_cache]}
            run_block(s_, "p", xp[s_, tok0:tok0 + BLK, :], yp[s_, tok0:tok0 + BLK, :],
                      kp_o[s_, tok0:tok0 + BLK, :], vp_o[s_, tok0:tok0 + BLK, :], tok0, BLK, blk * 4,
                      blk == 0, blk == T // BLK - 1, hp_o[s_], cp_o[s_], hooks=hk, ps_cap=cap,
                      direct_groups=((2, 3) if (s_ == 0 and blk == 0) else ()))

    chk('prompts')
    load(hcar, h0_d)
    load(ccar, conv0_d)
    run_block(2, "s", xsm, ysm, ks_o, vs_o, 0, TS, 16, True, True, hs_o, cs_o)

    kb.finish()
    return nc, kb


_CACHE = {}


def _rope_tables():
    half = 8
    inv_freq = (np.float32(500000.0) ** (-(np.arange(half, dtype=np.float32) * np.float32(2.0)) / np.float32(16))).astype(np.float32)
    pos = np.arange(NKT * 128, dtype=np.float32)
    ang = (pos[:, None] * inv_freq[None, :]).astype(np.float32)
    cos = np.cos(ang).astype(np.float32).reshape(NKT, 128, half).transpose(1, 0, 2)
    sin = np.sin(ang).astype(np.float32).reshape(NKT, 128, half).transpose(1, 0, 2)
    return np.ascontiguousarray(cos), np.ascontiguousarray(sin)


def kernel(x_prompt, x_sample, c_prompt, c_sample, cache_k, cache_v, state_lru_h, state_conv,
           w_ada, b_ada, w_in, g_q, g_k, lambda_q1, lambda_k1, lambda_q2, lambda_k2, g_subln,
           conv_w, conv_b, w_gate_a, b_gate_a, w_gate_x, b_gate_x, lru_lambda,
           w_out, w_up, w_down):
    f = lambda a: np.ascontiguousarray(np.asarray(a, dtype=np.float32))
    if "nc" not in _CACHE:
        _CACHE["nc"] = build()[0]
    nc = _CACHE["nc"]
    cos, sin = _rope_tables()
    fm = lambda v: f(np.asarray(v).reshape(-1, 128).T)
    shared = {
        "w_ada": f(w_ada[0]), "w_in": f(w_in[0]), "w_out": f(w_out[0]), "w_up": f(w_up[0]),
        "w_down": f(w_down[0]),
        "badaT": fm(b_ada[0]),
        "gq_b": f(np.tile(np.asarray(g_q[0]), (128, 4))),
        "gk_b": f(np.tile(np.asarray(g_k[0]), (128, 4))),
        "gsub_b": f(np.tile(np.asarray(g_subln[0]), (128, 1))),
        "lams_b": f(np.broadcast_to(np.stack([np.asarray(lambda_q1[0]), np.asarray(lambda_k1[0]),
                                               np.asarray(lambda_q2[0]), np.asarray(lambda_k2[0])])[None],
                                    (128, 4, 64))),
        "convw": f(np.asarray(conv_w[0]).reshape(4, 8, 128).transpose(2, 1, 0)),
        "convb": fm(conv_b[0]),
        "wga": f(w_gate_a[0]), "wgx": f(w_gate_x[0]),
        "bga": fm(np.asarray(b_gate_a[0]).reshape(-1)), "bgx": fm(np.asarray(b_gate_x[0]).reshape(-1)),
        "lam_lru": fm(lru_lambda[0]),
        "cos_t": cos, "sin_t": sin,
        "ident": np.eye(128, dtype=np.float32),
    }
    xpn = np.asarray(x_prompt)
    xsn = np.asarray(x_sample)
    in_maps = []
    for i in range(8):
        c3 = np.stack([np.asarray(c_prompt[2 * i]), np.asarray(c_prompt[2 * i + 1]), np.asarray(c_sample[i])])
        m = dict(shared)
        m["xp"] = f(xpn[2 * i:2 * i + 2])
        m["xs"] = f(xsn[i])
        m["cT"] = f(c3.reshape(3, 16, 128).transpose(2, 1, 0))
        m["ck"] = f(np.asarray(cache_k[0, i]).reshape(PAST, 1024))
        m["cv"] = f(np.asarray(cache_v[0, i]).reshape(PAST, 1024))
        m["h0"] = fm(state_lru_h[0, i])
        m["conv0"] = f(np.asarray(state_conv[0, i]).reshape(3, 8, 128).transpose(2, 1, 0))
        in_maps.append(m)
    res = run_bass_kernel_spmd(nc, in_maps, core_ids=list(range(8)))
    R = res.results
    y_p = np.concatenate([r["yp"] for r in R], axis=0)
    y_s = np.stack([r["ys"] for r in R], axis=0)
    k_p = np.concatenate([r["kp"] for r in R], axis=0).reshape(1, 16, T, NH, 2, 64)
    v_p = np.concatenate([r["vp"] for r in R], axis=0).reshape(1, 16, T, NH, 128)
    h_p = np.concatenate([r["hp"].transpose(0, 2, 1).reshape(NPS, 1024) for r in R], axis=0)[None]
    c_p = np.concatenate([r["cp"].transpose(0, 3, 2, 1).reshape(NPS, 3, 1024) for r in R], axis=0)[None]
    k_s = np.stack([r["ksn"] for r in R], axis=0).reshape(1, 8, TS, NH, 2, 64)
    v_s = np.stack([r["vsn"] for r in R], axis=0).reshape(1, 8, TS, NH, 128)
    h_s = np.stack([r["hsn"].T.reshape(1024) for r in R], axis=0)[None]
    c_s = np.stack([r["csn"].transpose(2, 1, 0).reshape(3, 1024) for r in R], axis=0)[None]
    outs = (y_p, y_s, k_p, v_p, h_p, c_p, k_s, v_s, h_s, c_s)
    return tuple(np.ascontiguousarray(o, dtype=np.float32) for o in outs)
```

```python
import numpy as np
import concourse.bass as bass
import concourse.mybir as mybir
from concourse.bass_utils import run_bass_kernel_spmd

F32 = mybir.dt.float32
BF16 = mybir.dt.bfloat16
AF = mybir.ActivationFunctionType
ALU = mybir.AluOpType
AX = mybir.AxisListType

D = 2048
T = 2048
NPS = 2
TS = 64
PAST = 2048
NH = 8
DFF = 8192
INC = 5120
EPS = 1e-6
LAM_INIT = 0.8 - 0.6 * 1.0
BLK = 512
NKT = 17
NSCR = 11
SCRW = 516


class Buf:
    __slots__ = ("w", "r", "n")

    def __init__(self, n):
        self.n = n
        self.w = [None] * n
        self.r = [dict() for _ in range(n)]


class Tl:
    def __init__(self, h, cells=1, buf=None, excl=False):
        self.h = h
        self.buf = buf if buf is not None else Buf(cells)
        self.excl = excl

    def __getitem__(self, idx):
        return self.h[idx]


def _cells(t, c):
    if c is None:
        return range(t.buf.n)
    if isinstance(c, int):
        return (c,)
    return c


class KB:
    NDMA = {"sp": 24, "pool": 40, "act": 24}

    def __init__(self, nc):
        self.nc = nc
        self.E = {"pe": nc.tensor, "act": nc.scalar, "dve": nc.vector, "pool": nc.gpsimd, "sp": nc.sync}
        self.sems = []
        self.esem = {}
        self.ecnt = {}
        for n in self.E:
            self.esem[n] = self._new_sem("e_" + n)
            self.ecnt[n] = 0
        self.waited = {n: {} for n in self.E}
        self.dsem = {q: [self._new_sem("d%s%d" % (q, i)) for i in range(n)] for q, n in self.NDMA.items()}
        self.dcnt = {q: [0] * n for q, n in self.NDMA.items()}
        self.drr = {q: 0 for q in self.NDMA}
        self.out_tokens = []
        self.ninst = {n: 0 for n in self.E}

    def _new_sem(self, name):
        self.sems.append(self.nc.alloc_semaphore(name))
        return len(self.sems) - 1

    def _wait(self, eng, s, v):
        if self.waited[eng].get(s, 0) >= v:
            return
        for n, es in self.esem.items():
            if es == s:
                assert v <= self.ecnt[n], "wait on a mark not yet emitted: %s waits %s >= %d (cur %d)" % (eng, n, v, self.ecnt[n])
        self.E[eng].wait_ge(self.sems[s], v)
        self.waited[eng][s] = v

    def _collect(self, eng, reads, writes):
        need = {}
        own = self.esem[eng]

        def add(s, v):
            if need.get(s, 0) < v:
                need[s] = v

        for (t, c) in reads:
            b = t.buf
            for i in _cells(t, c):
                tok = b.w[i]
                if tok is None:
                    continue
                if tok[0] == own and eng == "pe":
                    continue
                add(*tok)
            if t.excl:
                for i in _cells(t, c):
                    for s, v in b.r[i].items():
                        if s != own:
                            add(s, v)
        strict = eng in ("act", "dve", "pool")
        for (t, c) in writes:
            b = t.buf
            for i in _cells(t, c):
                tok = b.w[i]
                if tok is not None and (tok[0] != own or strict):
                    add(*tok)
                for s, v in b.r[i].items():
                    if s != own or strict:
                        add(s, v)
        for s, v in need.items():
            self._wait(eng, s, v)

    def _commit(self, tok, reads, writes):
        s, v = tok
        for (t, c) in reads:
            b = t.buf
            for i in _cells(t, c):
                if b.r[i].get(s, 0) < v:
                    b.r[i][s] = v
        for (t, c) in writes:
            b = t.buf
            for i in _cells(t, c):
                b.w[i] = tok
                b.r[i] = {}

    def op(self, eng, fn, reads, writes, mark=True, **kw):
        self._collect(eng, reads, writes)
        inst = fn(**kw)
        self.ninst[eng] += 1
        s = self.esem[eng]
        if mark:
            inst.then_inc(self.sems[s], 1)
            self.ecnt[eng] += 1
            tok = (s, self.ecnt[eng])
        else:
            tok = (s, self.ecnt[eng] + 1)
        self._commit(tok, reads, writes)
        return tok

    def dma(self, q, out, in_, reads, writes, is_output=False, **kw):
        self._collect(q, reads, writes)
        i = self.drr[q]
        self.drr[q] = (i + 1) % self.NDMA[q]
        s = self.dsem[q][i]
        prev = self.dcnt[q][i]
        if prev > 0:
            self._wait(q, s, prev)
        inst = self.E[q].dma_start(out=out, in_=in_, **kw)
        inst.then_inc(self.sems[s], 16)
        self.ninst[q] += 1
        self.dcnt[q][i] = prev + 16
        tok = (s, prev + 16)
        self._commit(tok, reads, writes)
        if is_output:
            self.out_tokens.append(tok)
        return tok

    def finish(self):
        for q, n in self.NDMA.items():
            for i in range(n):
                if self.dcnt[q][i] > 0:
                    self._wait("sp", self.dsem[q][i], self.dcnt[q][i])
        for n in ("pe", "act", "dve", "pool"):
            if self.ecnt[n] > 0:
                self._wait("sp", self.esem[n], self.ecnt[n])


class _Stop(Exception):
    pass


def build(stop=None):
    try:
        return _build(stop)
    except _Stop as e:
        nc, kb = e.args
        kb.finish()
        return nc, kb


def _build(stop=None):
    nc = bass.Bass("TRN2", target_bir_lowering=False)
    kb = KB(nc)

    kb.phase_log = []

    def chk(tag):
        kb.phase_log.append((tag, dict(kb.ninst)))
        if stop == tag:
            raise _Stop(nc, kb)

    def din(name, shape, dt=F32):
        return nc.dram_tensor(name, list(shape), dt, kind="ExternalInput").ap()

    def dout(name, shape):
        return nc.dram_tensor(name, list(shape), F32, kind="ExternalOutput").ap()

    xp = din("xp", [NPS, T, D])
    xsm = din("xs", [TS, D])
    cT_d = din("cT", [128, 16, 3])
    ck_d = din("ck", [PAST, 1024])
    cv_d = din("cv", [PAST, 1024])
    h0_d = din("h0", [128, 8])
    conv0_d = din("conv0", [128, 8, 3])
    w_ada = din("w_ada", [D, 6 * D])
    w_in = din("w_in", [D, INC])
    w_out = din("w_out", [D, D])
    w_up = din("w_up", [D, DFF])
    w_down = din("w_down", [DFF, D])
    badaT_d = din("badaT", [128, 96])
    gq_d = din("gq_b", [128, 256])
    gk_d = din("gk_b", [128, 256])
    gsub_d = din("gsub_b", [128, 128])
    lams_d = din("lams_b", [128, 4, 64])
    convw_d = din("convw", [128, 8, 4])
    convb_d = din("convb", [128, 8])
    wga_d = din("wga", [8, 128, 128])
    wgx_d = din("wgx", [8, 128, 128])
    bga_d = din("bga", [128, 8])
    bgx_d = din("bgx", [128, 8])
    lru_d = din("lam_lru", [128, 8])
    cos_d = din("cos_t", [128, NKT, 8])
    sin_d = din("sin_t", [128, NKT, 8])
    ident_d = din("ident", [128, 128])

    yp = dout("yp", [NPS, T, D])
    ysm = dout("ys", [TS, D])
    kp_o = dout("kp", [NPS, T, 1024])
    vp_o = dout("vp", [NPS, T, 1024])
    hp_o = dout("hp", [NPS, 128, 8])
    cp_o = dout("cp", [NPS, 128, 8, 3])
    ks_o = dout("ksn", [TS, 1024])
    vs_o = dout("vsn", [TS, 1024])
    hs_o = dout("hsn", [128, 8])
    cs_o = dout("csn", [128, 8, 3])

    def dscr(name, shape):
        t = Tl(nc.dram_tensor(name, list(shape), BF16, kind="Internal").ap(), cells=(shape[0] // 2048) * (shape[1] // 1024))
        t.ncb = shape[1] // 1024
        return t

    w_in_bf = dscr("w_in_bf", [D, INC])
    w_out_bf = dscr("w_out_bf", [D, D])
    w_up_bf = dscr("w_up_bf", [D, DFF])
    w_down_bf = dscr("w_down_bf", [DFF, D])

    def sb(name, shape, dt=F32, cells=1):
        return Tl(nc.alloc_sbuf_tensor("sb_" + name, list(shape), dt), cells)

    ring = [sb("ring%d" % i, [128, 16, 256], BF16) for i in range(4)]
    ring_i = [0]
    kT = sb("kT", [128, NH, NKT * 128], BF16, cells=NH * NKT)
    vaug = sb("vaug", [128, NKT, NH, 130], BF16, cells=NKT * NH)
    hT = sb("hT", [128, 16, BLK], BF16, cells=16 * 4)
    qT = sb("qT", [128, NH, BLK], BF16, cells=NH * 4)
    mixT = sb("mixT", [128, 16, BLK], BF16, cells=16 * 4)
    xT = sb("xT", [128, 16, BLK], F32, cells=16 * 4)
    modT = sb("modT", [128, 96, 3], F32)
    scr = [sb("scr%d" % i, [128, SCRW], F32, cells=2) for i in range(NSCR)]
    scr_i = [0]

    role_cnt = {}

    def role(name, idxs):
        i = role_cnt.get(name, 0)
        role_cnt[name] = i + 1
        return scr[idxs[i % len(idxs)]]

    def bfv(t, n):
        return t.h[:].bitcast(BF16)[:, 0:n]

    gq_b = sb("gq_b", [128, 256])
    gk_b = sb("gk_b", [128, 256])
    gsub_b = sb("gsub_b", [128, 128])
    lamv = sb("lamv", [128, 8])
    convw = sb("convw", [128, 8, 4])
    convb = sb("convb", [128, 8])
    wga = sb("wga", [128, 8, 128], BF16)
    wgx = sb("wgx", [128, 8, 128], BF16)
    bga = sb("bga", [128, 8])
    bgx = sb("bgx", [128, 8])
    lru = sb("lru", [128, 8])
    cneg = sb("cneg", [128, 8])
    cos_t = sb("cos_t", [128, NKT, 8])
    sin_t = sb("sin_t", [128, NKT, 8])
    ident = sb("ident", [128, 128])
    ident_bf = sb("ident_bf", [128, 128], BF16)
    ones_bf = sb("ones_bf", [128, 128], BF16)
    cT = sb("cT", [128, 16, 3])
    scT = sb("scT", [128, 16, 3], BF16)
    badaT = sb("badaT", [128, 96])
    hcar = sb("hcar", [128, 8])
    ccar = sb("ccar", [128, 8, 3])
    stat = sb("stat", [128, 16], cells=16)
    stat_i = [0]
    qstat = sb("qstat", [128, 64], cells=4)
    qs_i = [0]
    nsin_t = sb("nsin_t", [128, NKT, 8])
    neghalf = sb("neghalf", [128, 8])
    pts = [(Tl(scr[i].h[:].bitcast(BF16)[:, hh * 512:(hh + 1) * 512], buf=scr[i].buf), hh)
           for i in range(4) for hh in range(2)]
    pt_i = [0]

    xst = [Tl(mixT.h[:, 8 * i:8 * i + 8, :].rearrange("p a b -> p (a b)").bitcast(F32), buf=mixT.buf) for i in range(2)]
    xst_cells = [range(32 * i, 32 * i + 32) for i in range(2)]
    xn_cells = range(0, 16)

    ps = [Tl(nc.alloc_psum_tensor("ps%d" % i, [128, 512], F32), excl=True) for i in range(8)]
    ps_i = [0]
    ps_lim = [8]

    def nps():
        t = ps[ps_i[0] % ps_lim[0]]
        ps_i[0] += 1
        return t

    E = kb.E
    pe, act, dve, pool = E["pe"], E["act"], E["dve"], E["pool"]

    def A(t, c=None):
        return (t, c)

    def mm(out, lhsT, rhs, start, stop, reads, writes, mark=None):
        return kb.op("pe", pe.matmul, reads, writes, mark=(stop if mark is None else mark), out=out, lhsT=lhsT,
                     rhs=rhs, start=start, stop=stop)

    def tr(out, in_, idt, reads, writes, mark=True):
        return kb.op("pe", pe.transpose, reads, writes, mark=mark, out=out, in_=in_, identity=idt)

    def actf(out, in_, func, reads, writes, **kw):
        return kb.op("act", act.activation, reads, writes, out=out, in_=in_, func=func, **kw)

    def ts(eng, out, in0, s1, s2, op0, op1, reads, writes):
        if op1 is None:
            return kb.op(eng, E[eng].tensor_scalar, reads, writes, out=out, in0=in0, scalar1=s1,
                         scalar2=None, op0=op0)
        return kb.op(eng, E[eng].tensor_scalar, reads, writes, out=out, in0=in0, scalar1=s1,
                     scalar2=s2, op0=op0, op1=op1)

    def tt(eng, out, in0, in1, op, reads, writes):
        return kb.op(eng, E[eng].tensor_tensor, reads, writes, out=out, in0=in0, in1=in1, op=op)

    def stt(out, in0, scalar, in1, op0, op1, reads, writes):
        return kb.op("dve", dve.scalar_tensor_tensor, reads, writes, out=out, in0=in0, scalar=scalar,
                     in1=in1, op0=op0, op1=op1)

    def cp(eng, out, in_, reads, writes):
        if eng == "act":
            return kb.op("act", act.copy, reads, writes, out=out, in_=in_)
        return kb.op(eng, E[eng].tensor_copy, reads, writes, out=out, in_=in_)

    def load(t, src, q="sp", cells=None, reads=()):
        return kb.dma(q, t.h[:] if isinstance(t, Tl) else t, src, list(reads), [A(t, cells)])

    lams = Tl(scr[10].h[:, 0:256].rearrange("p (a b) -> p a b", a=4), buf=scr[10].buf)
    for t, d in ((gq_b, gq_d), (gk_b, gk_d), (gsub_b, gsub_d), (lams, lams_d), (convw, convw_d),
                 (convb, convb_d), (bga, bga_d), (bgx, bgx_d), (lru, lru_d), (cos_t, cos_d),
                 (sin_t, sin_d), (ident, ident_d), (cT, cT_d), (badaT, badaT_d)):
        load(t, d)
    kb.dma("pool", wga.h[:], wga_d.rearrange("n d e -> d n e"), [], [A(wga)])
    kb.dma("pool", wgx.h[:], wgx_d.rearrange("n d e -> d n e"), [], [A(wgx)])

    cp("act", ident_bf.h[:], ident.h[:], [A(ident)], [A(ident_bf)])
    kb.op("dve", dve.memset, [], [A(ones_bf)], ap=ones_bf.h[:], constant=1.0)
    kb.op("dve", dve.memset, [], [A(neghalf)], ap=neghalf.h[:], constant=-0.5)
    kb.op("dve", dve.memset, [], [A(vaug)], ap=vaug.h[:, :, :, 128:130], constant=1.0)
    ts("dve", nsin_t.h[:], sin_t.h[:], -1.0, None, ALU.mult, None, [A(sin_t)], [A(nsin_t)])

    def load_sample_cache():
        for j in range(16):
            cells = [j * NH + h for h in range(NH)]
            kb.dma("pool", vaug.h[:, j, :, 0:128],
                   cv_d[j * 128:(j + 1) * 128, :].rearrange("p (h d) -> p h d", h=NH),
                   [], [A(vaug, cells)])
        for j in range(16):
            kst_t = role("k_a", [0, 2])
            kst2_t = role("k_b", [1, 3])
            for hf, tl in ((0, kst_t), (1, kst2_t)):
                kb.dma("pool", bfv(tl, 512), ck_d[j * 128:(j + 1) * 128, hf * 512:(hf + 1) * 512], [], [A(tl)])
            p_ = nps()
            pb = p_.h[:].bitcast(BF16)
            for h in range(NH):
                tl = kst_t if h < 4 else kst2_t
                tr(pb[:, h * 128:(h + 1) * 128], bfv(tl, 512)[:, (h % 4) * 128:(h % 4 + 1) * 128], ident_bf.h[:],
                   [A(tl), A(ident_bf)], [A(p_)], mark=(h == NH - 1))
            cells = [h * NKT + j for h in range(NH)]
            cp("dve" if j % 2 else "act", kT.h[:, :, j * 128:(j + 1) * 128],
               pb[:, 0:1024].rearrange("p (a b) -> p a b", a=NH), [A(p_)], [A(kT, cells)])


    s0 = scr[9]
    tt("dve", s0.h[:, 0:64], lams.h[:, 0, :], lams.h[:, 1, :], ALU.mult, [A(lams)], [A(s0)])
    tt("dve", s0.h[:, 64:128], lams.h[:, 2, :], lams.h[:, 3, :], ALU.mult, [A(lams)], [A(s0)])
    kb.op("dve", dve.tensor_reduce, [A(s0)], [A(lamv)], out=lamv.h[:, 0:2],
          in_=s0.h[:, 0:128].rearrange("p (a b) -> p a b", a=2), axis=AX.X, op=ALU.add)
    actf(lamv.h[:, 2:4], lamv.h[:, 0:2], AF.Exp, [A(lamv)], [A(lamv)])
    tt("dve", lamv.h[:, 4:5], lamv.h[:, 2:3], lamv.h[:, 3:4], ALU.subtract, [A(lamv)], [A(lamv)])
    ts("dve", lamv.h[:, 5:6], lamv.h[:, 4:5], LAM_INIT, -1.0, ALU.add, ALU.mult, [A(lamv)], [A(lamv)])
    neglam = lamv.h[:, 5:6]
    ts("dve", gsub_b.h[:], gsub_b.h[:], 1.0 - LAM_INIT, None, ALU.mult, None, [A(gsub_b)], [A(gsub_b)])
    actf(cneg.h[:], lru.h[:], AF.Exp, [A(lru)], [A(cneg)], scale=-1.0)
    actf(cneg.h[:], cneg.h[:], AF.Ln, [A(cneg)], [A(cneg)], bias=1.0)
    ts("dve", cneg.h[:], cneg.h[:], -8.0, None, ALU.mult, None, [A(cneg)], [A(cneg)])
    actf(scT.h[:], cT.h[:], AF.Silu, [A(cT)], [A(scT)])

    chk('consts')
    ring_free = [0, 1, 2, 3]
    for i_, r_ in enumerate(ring):
        r_.slot = i_

    def ring_next():
        assert ring_free, "weight ring exhausted"
        return ring[ring_free.pop(0)]

    def wfree(t):
        assert t.slot not in ring_free
        ring_free.append(t.slot)

    def wload(src_tl, r0, c0, q="sp", src_ap=None, cell=None):
        t = ring_next()
        if src_ap is None:
            src = src_tl.h[r0:r0 + 2048, c0:c0 + 256].rearrange("(k p) n -> p k n", p=128)
            kb.dma(q, t.h[:], src, [A(src_tl, (r0 // 2048) * src_tl.ncb + c0 // 1024)], [A(t)])
        else:
            src = src_ap[r0:r0 + 2048, c0:c0 + 256].rearrange("(k p) n -> p k n", p=128)
            kb.dma(q, t.h[:], src, [], [A(t)])
        return t

    def mod_chunks(mps_t, cc0, cc1, base=0):
        for cc in range(cc0, cc1):
            wt = wload(None, 0, cc * 256, q="pool", src_ap=w_ada)
            for f in range(2):
                ci = cc * 2 + f
                for kc in range(16):
                    mm(mps_t.h[:, (ci - base) * 3:(ci - base) * 3 + 3], wt.h[:, kc, f * 128:(f + 1) * 128],
                       scT.h[:, kc, :], kc == 0, kc == 15, [A(wt), A(scT)], [A(mps_t)])
            wfree(wt)

    def mod_evac(mps_t, ci0, ci1, base=0):
        n = ci1 - ci0
        tt("dve", modT.h[:, ci0:ci1, :], mps_t.h[:, (ci0 - base) * 3:(ci0 - base) * 3 + n * 3].rearrange("p (c b) -> p c b", b=3),
           badaT.h[:, ci0:ci1].unsqueeze(2).broadcast_to([128, n, 3]), ALU.add, [A(mps_t), A(badaT)], [A(modT)])

    mps = nps()
    mod_chunks(mps, 0, 16)
    mod_evac(mps, 0, 32)
    ts("dve", modT.h[:, 16:32, :], modT.h[:, 16:32, :], 1.0, None, ALU.add, None, [A(modT)], [A(modT)])
    mps2 = ps[7]

    def mod(m, fc, b):
        return modT.h[:, m * 16 + fc, b:b + 1]

    chk('mods')

    def conv_blk(dst, src, rg, cb):
        kb.dma("pool", dst.h[rg * 2048:(rg + 1) * 2048, cb * 1024:(cb + 1) * 1024],
               src[rg * 2048:(rg + 1) * 2048, cb * 1024:(cb + 1) * 1024], [], [A(dst, rg * dst.ncb + cb)])

    def cv_in(cb):
        return lambda: conv_blk(w_in_bf, w_in, 0, cb)

    def cv_out(cb):
        return lambda: conv_blk(w_out_bf, w_out, 0, cb)

    def cv_up(g):
        return lambda: [conv_blk(w_up_bf, w_up, 0, g * 2 + cb) for cb in range(2)]

    def cv_down(g):
        return lambda: [conv_blk(w_down_bf, w_down, g, cb) for cb in range(2)]

    def mods2(cc0, cc1):
        return lambda: mod_chunks(mps2, cc0, cc1, 32)

    def mods2_fin():
        mod_evac(mps2, 32, 96, 32)
        ts("dve", modT.h[:, 64:80, :], modT.h[:, 64:80, :], 1.0, None, ALU.add, None, [A(modT)], [A(modT)])

    conv_blk(w_in_bf, w_in, 0, 0)
    conv_blk(w_in_bf, w_in, 0, 1)
    first_hooks = {
        ("Q", 0): [cv_in(2), mods2(16, 22)], ("Q", 1): [cv_in(3), mods2(22, 28)], ("Q", 2): [cv_in(4), mods2(28, 34)],
        ("Q", 3): [cv_out(0), mods2(34, 40)], ("Q", 4): [cv_out(1), mods2(40, 44)],
        ("Q", 5): [cv_up(0), mods2(44, 48), mods2_fin],
        ("R", 0): [cv_down(0)], ("A", 0): [cv_up(1)], ("A", 1): [cv_down(1)],
    }
    second_hooks = {("Q", 1): [cv_up(2)], ("Q", 3): [cv_down(2)], ("A", 0): [cv_up(3)], ("A", 1): [cv_down(3)]}

    chk('convert')
    def run_block(b, seq_kind, x_src, y_dst, k_dst, v_dst, tok0, ntok, kt0, first, last, h_dst, c_dst,
                  hooks=None, ps_cap=8, direct_groups=()):
        R = min(128, ntok)
        ntile = (ntok + 127) // 128
        NT = ntok
        hooks = hooks or {}
        allt = lambda fc: [fc * 4 + t_ for t_ in range(ntile)]

        def hook(tag, i):
            for fn in hooks.get((tag, i), ()):
                fn()

        def st_col():
            i = stat_i[0] % 16
            stat_i[0] += 1
            return stat.h[:, i:i + 1], i

        ps_lim[0] = ps_cap

        def l_stage1(t):
            xs_ = xst[t % 2]
            xc = xst_cells[t % 2]
            kb.dma("sp", xs_.h[0:R, :], x_src[t * 128:t * 128 + R, :], [], [A(xs_, xc)])
            xn = qT.h[0:R, 4 * (t % 2):4 * (t % 2) + 4, :].rearrange("p a b -> p (a b)")
            xnc = range(16 * (t % 2), 16 * (t % 2) + 16)
            ss, ssc = st_col()
            rs, rsc = st_col()
            kb.op("act", act.activation, [A(xs_, xc)], [A(qT, xnc), A(stat, ssc)], out=xn, in_=xs_.h[0:R, :],
                  func=AF.Square, accum_out=ss[0:R, :])
            actf(rs[0:R, :], ss[0:R, :], AF.Sqrt, [A(stat, ssc)], [A(stat, rsc)], scale=1.0 / D, bias=EPS)
            kb.op("dve", dve.reciprocal, [A(stat, rsc)], [A(stat, rsc)], out=rs[0:R, :], in_=rs[0:R, :])
            ts("dve", xn, xs_.h[0:R, :], rs[0:R, :], None, ALU.mult, None, [A(xs_, xc), A(stat, rsc)],
               [A(qT, xnc)])
            return xs_, xc, xn, xnc

        def l_stage2(t, xs_, xc, xn, xnc):
            for g in range(4):
                p_ = nps()
                for j in range(4):
                    fc = g * 4 + j
                    tr(p_.h[:, j * R:(j + 1) * R], xs_.h[0:R, fc * 128:(fc + 1) * 128], ident.h[0:R, 0:R],
                       [A(xs_, xc), A(ident)], [A(p_)], mark=(j == 3))
                cells = [fc * 4 + t for fc in range(g * 4, g * 4 + 4)]
                src = p_.h[:, 0:4 * R].rearrange("p (a b) -> p a b", a=4)
                dstv = xT.h[:, g * 4:g * 4 + 4, t * 128:t * 128 + R]
                if g % 2 == 0:
                    cp("act", dstv, src, [A(p_)], [A(xT, cells)])
                else:
                    cp("dve", dstv, src, [A(p_)], [A(xT, cells)])
            for g in range(2):
                p_ = nps()
                pb = p_.h[:].bitcast(BF16)
                for j in range(8):
                    fc = g * 8 + j
                    tr(pb[:, j * R:(j + 1) * R], xn[:, fc * 128:(fc + 1) * 128], ident_bf.h[0:R, 0:R],
                       [A(qT, xnc), A(ident_bf)], [A(p_)], mark=(j == 7))
                for j in range(8):
                    fc = g * 8 + j
                    if g == 0:
                        actf(hT.h[:, fc, t * 128:t * 128 + R], pb[:, j * R:(j + 1) * R], AF.Identity,
                             [A(p_), A(modT)], [A(hT, fc * 4 + t)], scale=mod(1, fc, b), bias=mod(0, fc, b))
                    else:
                        ts("dve", hT.h[:, fc, t * 128:t * 128 + R], pb[:, j * R:(j + 1) * R], mod(1, fc, b),
                           mod(0, fc, b), ALU.mult, ALU.add, [A(p_), A(modT)], [A(hT, fc * 4 + t)])

        l_ctx = {0: l_stage1(0)}
        for t in range(ntile):
            if t + 1 < ntile:
                l_ctx[t + 1] = l_stage1(t + 1)
            l_stage2(t, *l_ctx[t])

        chk('L')
        def q_s1(pp):
            wA = wload(w_in_bf, 0, (2 * pp) * 256)
            wB = wload(w_in_bf, 0, (2 * pp + 1) * 256)
            kind = pp // 2
            h0_ = (pp % 2) * 4
            ctxs = []
            for t in range(ntile):
                p_ = nps()
                for half, wt in ((0, wA), (1, wB)):
                    for kc in range(16):
                        mm(p_.h[0:R, half * 256:(half + 1) * 256], hT.h[:, kc, t * 128:t * 128 + R], wt.h[:, kc, :],
                           kc == 0, kc == 15, [A(hT, kc * 4 + t), A(wt)], [A(p_)])
                pv = p_.h[0:R, :]
                if kind == 2:
                    vst = role("q_vst", [0, 1, 2])
                    cp("act", vst.h[0:R, 0:512], pv, [A(p_)], [A(vst)])
                    kb.dma("act", v_dst[t * 128:t * 128 + R, h0_ * 128:(h0_ + 4) * 128], vst.h[0:R, 0:512],
                           [A(vst)], [], is_output=True)
                    cells = [(kt0 + t) * NH + h0_ + j for j in range(4)]
                    cp("dve", vaug.h[0:R, kt0 + t, h0_:h0_ + 4, 0:128],
                       pv.rearrange("p (a b) -> p a b", a=4), [A(p_)], [A(vaug, cells)])
                    continue
                sqa = role("q_vst", [0, 1, 2])
                sq = role("q_sq", [3, 4, 5, 9])
                qk = role("q_qk", [6, 7, 8, 10])
                si = qs_i[0] % 4
                qs_i[0] += 1
                ss = qstat.h[0:R, si * 16:si * 16 + 8]
                rs = qstat.h[0:R, si * 16 + 8:si * 16 + 16]
                actf(sqa.h[0:R, 0:512], pv, AF.Square, [A(p_)], [A(sqa)])
                kb.op("dve", dve.tensor_reduce, [A(sqa)], [A(qstat, si)], out=ss,
                      in_=sqa.h[0:R, 0:512].rearrange("p (a b) -> p a b", a=8), axis=AX.X, op=ALU.add)
                ts("pool", rs, ss, 1.0 / 64, EPS, ALU.mult, ALU.add, [A(qstat, si)], [A(qstat, si)])
                tt("pool", rs, rs, neghalf.h[0:R, 0:8], ALU.pow, [A(qstat, si), A(neghalf)], [A(qstat, si)])
                ctxs.append((t, p_, sq, qk, si, rs))
            wfree(wA)
            wfree(wB)
            return pp, kind, h0_, ctxs

        def q_s2(st):
            pp, kind, h0_, ctxs = st
            gb_ = gq_b if kind == 0 else gk_b
            for (t, p_, sq, qk, si, rs) in ctxs:
                pv = p_.h[0:R, :]
                q8 = qk.h[0:R, 0:512].rearrange("p (a b) -> p a b", a=8)
                tt("dve", q8, pv.rearrange("p (a b) -> p a b", a=8), rs.unsqueeze(2).broadcast_to([R, 8, 64]),
                   ALU.mult, [A(p_), A(qstat, si)], [A(qk)])
                q2 = qk.h[0:R, 0:512].rearrange("p (a b) -> p a b", a=2)
                tt("pool", q2, q2, gb_.h[0:R, :].unsqueeze(1).broadcast_to([R, 2, 256]), ALU.mult,
                   [A(qk), A(gb_)], [A(qk)])

        def q_s34(st):
            pp, kind, h0_, ctxs = st
            p2s = {}
            for (t, p_, sq, qk, si, rs) in ctxs:
                q8 = qk.h[0:R, 0:512].rearrange("p (a b) -> p a b", a=8)
                x16 = q8[:, :, 0:16]
                x1 = q8[:, :, 0:8]
                x2 = q8[:, :, 8:16]
                tile_idx = kt0 + t
                cos4 = cos_t.h[0:R, tile_idx, :].unsqueeze(1).unsqueeze(1).broadcast_to([R, 8, 2, 8])
                sinb = sin_t.h[0:R, tile_idx, :].unsqueeze(1).broadcast_to([R, 8, 8])
                nsinb = nsin_t.h[0:R, tile_idx, :].unsqueeze(1).broadcast_to([R, 8, 8])
                tm = sq.h[0:R, 256:384].rearrange("p (a b) -> p a b", a=8)
                tt("dve", tm[:, :, 0:8], x2, nsinb, ALU.mult, [A(qk), A(nsin_t)], [A(sq)])
                tt("dve", tm[:, :, 8:16], x1, sinb, ALU.mult, [A(qk), A(sin_t)], [A(sq)])
                x16v = x16.rearrange("p g (a b) -> p g a b", a=2)
                tt("dve", x16v, x16v, cos4, ALU.mult, [A(qk), A(cos_t)], [A(qk)])
                tt("dve", x16, x16, tm, ALU.add, [A(qk), A(sq)], [A(qk)])
                if kind == 1:
                    kb.dma("act", k_dst[t * 128:t * 128 + R, h0_ * 128:(h0_ + 4) * 128], qk.h[0:R, 0:512],
                           [A(qk)], [], is_output=True)
                qb = bfv(sq, 512)
                cp("act", qb[0:R, :], qk.h[0:R, 0:512], [A(qk)], [A(sq)])
                p2 = nps()
                p2s[t] = p2
                pb = p2.h[:].bitcast(BF16)
                for j in range(4):
                    tr(pb[:, j * R:(j + 1) * R], qb[0:R, j * 128:(j + 1) * 128], ident_bf.h[0:R, 0:R],
                       [A(sq), A(ident_bf)], [A(p2)], mark=(j == 3))
            for (t, p_, sq, qk, si, rs) in ctxs:
                p2 = p2s[t]
                pb = p2.h[:].bitcast(BF16)
                src = pb[:, 0:4 * R].rearrange("p (a b) -> p a b", a=4)
                if kind == 0:
                    cells = [(h0_ + j) * 4 + t for j in range(4)]
                    cp("dve", qT.h[:, h0_:h0_ + 4, t * 128:t * 128 + R], src, [A(p2)], [A(qT, cells)])
                else:
                    cells = [(h0_ + j) * NKT + kt0 + t for j in range(4)]
                    cp("dve", kT.h[:, h0_:h0_ + 4, (kt0 + t) * 128:(kt0 + t) * 128 + R], src, [A(p2)],
                       [A(kT, cells)])
            hook('Q', pp)

        if ps_cap == 8:
            q_prev = q_s1(0)
            q_s2(q_prev)
            for pp in range(1, 6):
                q_cur = q_s1(pp)
                q_s34(q_prev)
                q_s2(q_cur)
                q_prev = q_cur
            q_s34(q_prev)
        else:
            for pp in range(6):
                q_cur = q_s1(pp)
                q_s2(q_cur)
                q_s34(q_cur)

        chk('Q')
        rw = {}

        def r_weights(pr):
            if pr not in rw:
                rw[pr] = (wload(w_in_bf, 0, 3072 + pr * 256), wload(w_in_bf, 0, 4096 + pr * 256))
            return rw[pr]

        def r_stage_a(fc):
            wx, _ = r_weights(fc // 2)
            f = fc % 2
            p_ = nps()
            for kc in range(16):
                mm(p_.h[:, 0:NT], wx.h[:, kc, f * 128:(f + 1) * 128], hT.h[:, kc, 0:NT], kc == 0, kc == 15,
                   [A(hT, allt(kc)), A(wx)], [A(p_)])
            xe = role("r_xe", [0, 1])
            cp("act", xe.h[:, 3:3 + NT], p_.h[:, 0:NT], [A(p_)], [A(xe)])
            cp("pool", xe.h[:, 0:3], ccar.h[:, fc, :], [A(ccar)], [A(xe)])
            cp("pool", ccar.h[:, fc, :], xe.h[:, NT:NT + 3], [A(xe)], [A(ccar)])
            u = role("r_u", [2, 9])
            ts("dve", u.h[:, 0:NT], xe.h[:, 0:NT], convw.h[:, fc, 0:1], convb.h[:, fc:fc + 1], ALU.mult, ALU.add,
               [A(xe), A(convw), A(convb)], [A(u)])
            for j in range(1, 4):
                stt(u.h[:, 0:NT], xe.h[:, j:j + NT], convw.h[:, fc, j:j + 1], u.h[:, 0:NT], ALU.mult, ALU.add,
                    [A(xe), A(convw), A(u)], [A(u)])
            ub_t = role("r_ub", [3, 10])
            cp("pool", bfv(ub_t, NT), u.h[:, 0:NT], [A(u)], [A(ub_t)])
            if f == 1:
                wfree(wx)
            return u, ub_t

        def r_stage_b(fc, u, ub_t):
            _, wg = r_weights(fc // 2)
            f = fc % 2
            ub = bfv(ub_t, NT)
            pa = nps()
            mm(pa.h[:, 0:NT], wga.h[:, fc, :], ub, True, True, [A(wga), A(ub_t)], [A(pa)])
            pi = nps()
            mm(pi.h[:, 0:NT], wgx.h[:, fc, :], ub, True, True, [A(wgx), A(ub_t)], [A(pi)])
            pg = nps()
            for kc in range(16):
                mm(pg.h[:, 0:NT], wg.h[:, kc, f * 128:(f + 1) * 128], hT.h[:, kc, 0:NT], kc == 0, kc == 15,
                   [A(hT, allt(kc)), A(wg)], [A(pg)])
            ra = role("r_ra", [4])
            ig = role("r_ig", [5])
            actf(ra.h[:, 0:NT], pa.h[:, 0:NT], AF.Sigmoid, [A(pa), A(bga)], [A(ra)], bias=bga.h[:, fc:fc + 1])
            actf(ig.h[:, 0:NT], pi.h[:, 0:NT], AF.Sigmoid, [A(pi), A(bgx)], [A(ig)], bias=bgx.h[:, fc:fc + 1])
            actf(ra.h[:, 0:NT], ra.h[:, 0:NT], AF.Exp, [A(ra), A(cneg)], [A(ra)], scale=cneg.h[:, fc:fc + 1])
            a2 = role("r_a2", [6])
            tt("dve", a2.h[:, 0:NT], ra.h[:, 0:NT], ra.h[:, 0:NT], ALU.mult, [A(ra)], [A(a2)])
            actf(a2.h[:, 0:NT], a2.h[:, 0:NT], AF.Sqrt, [A(a2)], [A(a2)], scale=-1.0, bias=1.0)
            tt("dve", ig.h[:, 0:NT], ig.h[:, 0:NT], u.h[:, 0:NT], ALU.mult, [A(ig), A(u)], [A(ig)])
            tt("dve", ig.h[:, 0:NT], ig.h[:, 0:NT], a2.h[:, 0:NT], ALU.mult, [A(ig), A(a2)], [A(ig)])
            hsq = role("r_hs", [7])
            kb.op("dve", dve.tensor_tensor_scan, [A(ra), A(ig), A(hcar)], [A(hsq)], out=hsq.h[:, 0:NT],
                  data0=ra.h[:, 0:NT], data1=ig.h[:, 0:NT], initial=hcar.h[:, fc:fc + 1], op0=ALU.mult,
                  op1=ALU.add)
            cp("pool", hcar.h[:, fc:fc + 1], hsq.h[:, NT - 1:NT], [A(hsq)], [A(hcar)])
            gl = role("r_gl", [8])
            actf(gl.h[:, 0:NT], pg.h[:, 0:NT], AF.Gelu_apprx_tanh, [A(pg)], [A(gl)])
            cells = [(8 + fc) * 4 + t for t in range(ntile)]
            tt("pool", mixT.h[:, 8 + fc, 0:NT], hsq.h[:, 0:NT], gl.h[:, 0:NT], ALU.mult, [A(hsq), A(gl)],
               [A(mixT, cells)])
            if f == 1:
                wfree(wg)

        hook('R', 0)
        nxt = r_stage_a(0)
        for fc in range(8):
            cur = nxt
            if fc + 1 < 8:
                nxt = r_stage_a(fc + 1)
            r_stage_b(fc, *cur)
            hook('Rf', fc)
        if last:
            kb.dma("act", h_dst, hcar.h[:], [A(hcar)], [], is_output=True)
            kb.dma("act", c_dst, ccar.h[:], [A(ccar)], [], is_output=True)

        chk('R')
        ps_lim[0] = 4
        nkt = kt0 + ntile
        oacc = ps[4:8]
        units = [(h, c, j) for h in range(NH) for c in range(2) for j in range(nkt)]
        LOOK = 3
        hstate = {}
        deferred = []
        cur_idx = [0]

        qz = [hT.h[:, 2 * i:2 * i + 2, :] for i in range(2)]
        qz_cells = [range(8 * i, 8 * i + 8) for i in range(2)]
        for i in range(2):
            kb.op("pool", pool.memset, [], [A(hT, qz_cells[i])], ap=qz[i][64:128, 0, 0:NT], constant=0.0)
            kb.op("pool", pool.memset, [], [A(hT, qz_cells[i])], ap=qz[i][0:64, 1, 0:NT], constant=0.0)

        def load_qz(h):
            i = h % 2
            qc = [h * 4 + t for t in range(ntile)]
            cp("pool", qz[i][0:64, 0, 0:NT], qT.h[0:64, h, 0:NT], [A(qT, qc)], [A(hT, qz_cells[i])])
            cp("pool", qz[i][64:128, 1, 0:NT], qT.h[64:128, h, 0:NT], [A(qT, qc)], [A(hT, qz_cells[i])])

        def s_stage(h, c, j):
            kr = 128 if (seq_kind == "p" or j < 16) else TS
            q0 = (j - kt0) * 128 if (seq_kind == "p" and j >= kt0) else 0
            if c == 0 and j == 0:
                if h == 0:
                    load_qz(0)
                if h + 1 < NH:
                    load_qz(h + 1)
            sp_ = nps()
            mm(sp_.h[0:kr, q0:NT], kT.h[:, h, j * 128:j * 128 + kr], qz[h % 2][:, c, q0:NT], True, True,
               [A(kT, h * NKT + j), A(hT, qz_cells[h % 2])], [A(sp_)])
            pt_tl, pc = pts[pt_i[0] % 8]
            pt_i[0] += 1
            pt = pt_tl.h
            actf(pt[0:kr, q0:NT], sp_.h[0:kr, q0:NT], AF.Exp, [A(sp_)], [A(pt_tl, pc)], scale=0.125)
            if seq_kind == "p" and j >= kt0:
                kb.op("pool", pool.memset, [], [A(pt_tl, pc)], ap=pt[64:128, q0:q0 + 64], constant=0.0)
            return pt_tl, pc, kr, q0

        def evac_t(h, c, t):
            if h not in hstate:
                hstate[h] = (role("a_osb", [4, 5]), role("a_rsum", [6, 7]))
            osb, rsum = hstate[h]
            rc = rsum.h[0:R, (c * 4 + t):(c * 4 + t) + 1]
            kb.op("dve", dve.reciprocal, [A(oacc[t])], [A(rsum)], out=rc, in_=oacc[t].h[0:R, 128:129])
            ov = osb.h[0:R, t * 128:(t + 1) * 128]
            if c == 0:
                ts("dve", ov, oacc[t].h[0:R, 0:128], rc, None, ALU.mult, None, [A(oacc[t]), A(rsum)], [A(osb)])
            else:
                ts("dve", rc, rc, neglam[0:R, :], None, ALU.mult, None, [A(rsum), A(lamv)], [A(rsum)])
                stt(ov, oacc[t].h[0:R, 0:128], rc, ov, ALU.mult, ALU.add, [A(oacc[t]), A(rsum), A(osb)], [A(osb)])

        def head_part2(h, ob_t):
            ob = bfv(ob_t, 512)
            p2 = nps()
            pb = p2.h[:].bitcast(BF16)
            for t in range(ntile):
                tr(pb[:, t * R:(t + 1) * R], ob[0:R, t * 128:(t + 1) * 128], ident_bf.h[0:R, 0:R],
                   [A(ob_t), A(ident_bf)], [A(p2)], mark=True)
            cells = [h * 4 + t for t in range(ntile)]
            cp("dve", mixT.h[:, h, 0:NT], pb[:, 0:NT], [A(p2)], [A(mixT, cells)])

        def finish_head(h):
            osb, rsum = hstate[h]
            sqj = role("a_sqj", [8])
            ob_t = role("a_ob", [9, 10])
            ob = bfv(ob_t, 512)
            for t in range(ntile):
                ov = osb.h[0:R, t * 128:(t + 1) * 128]
                kb.op("dve", dve.scalar_tensor_tensor, [A(osb)], [A(sqj), A(rsum)], out=sqj.h[0:R, 0:128], in0=ov,
                      scalar=1.0, in1=ov, op0=ALU.mult, op1=ALU.mult, accum_out=rsum.h[0:R, 8 + t:9 + t])
            ts("pool", rsum.h[0:R, 12:12 + ntile], rsum.h[0:R, 8:8 + ntile], 1.0 / 128, EPS, ALU.mult, ALU.add,
               [A(rsum)], [A(rsum)])
            tt("pool", rsum.h[0:R, 12:12 + ntile], rsum.h[0:R, 12:12 + ntile], neghalf.h[0:R, 0:ntile], ALU.pow,
               [A(rsum), A(neghalf)], [A(rsum)])
            for t in range(ntile):
                ov = osb.h[0:R, t * 128:(t + 1) * 128]
                stt(ob[0:R, t * 128:(t + 1) * 128], ov, rsum.h[0:R, 12 + t:13 + t], gsub_b.h[0:R, :], ALU.mult,
                    ALU.mult, [A(osb), A(rsum), A(gsub_b)], [A(ob_t)])
            deferred.append((cur_idx[0] + 10, h, ob_t))

        def pv_stage(h, c, j, info):
            pt_tl, pc, kr, q0 = info
            pt = pt_tl.h
            for t in range(q0 // 128, ntile):
                jl = (kt0 + t) if seq_kind == "p" else (nkt - 1)
                mm(oacc[t].h[0:128, 0:130], pt[0:kr, t * 128:t * 128 + 128], vaug.h[0:kr, j, h, :],
                   j == 0, j == jl, [A(pt_tl, pc), A(vaug, j * NH + h)], [A(oacc[t])])
                if j == jl:
                    evac_t(h, c, t)
            if j == nkt - 1 and c == 1:
                finish_head(h)

        def run_deferred(force=False):
            while deferred and (force or deferred[0][0] <= cur_idx[0]):
                _, h, ob_t = deferred.pop(0)
                head_part2(h, ob_t)

        hook('A', 0)
        pend = []
        for idx, u_ in enumerate(units):
            cur_idx[0] = idx
            if idx == len(units) // 2:
                hook('A', 1)
            pend.append((u_, s_stage(*u_)))
            if len(pend) > LOOK:
                uu, inf = pend.pop(0)
                pv_stage(*uu, inf)
            run_deferred()
        while pend:
            uu, inf = pend.pop(0)
            pv_stage(*uu, inf)
        run_deferred(force=True)
        ps_lim[0] = 8
        hook('O', 0)

        chk('A')
        hook('N', 0)
        ps_lim[0] = 7
        pn = ps[7]
        pend_sq = []

        def ones_mm(fc, sq_t):
            mm(pn.h[:, 0:NT], ones_bf.h[:], bfv(sq_t, NT), fc == 0, fc == 15, [A(ones_bf), A(sq_t)], [A(pn)], mark=True)

        for cc in range(8):
            wt = wload(w_out_bf, 0, cc * 256)
            for f in range(2):
                fc = cc * 2 + f
                p_ = nps()
                for kc in range(16):
                    mm(p_.h[:, 0:NT], wt.h[:, kc, f * 128:(f + 1) * 128], mixT.h[:, kc, 0:NT], kc == 0, kc == 15,
                       [A(mixT, allt(kc)), A(wt)], [A(p_)])
                if pend_sq:
                    ones_mm(*pend_sq.pop(0))
                stt(xT.h[:, fc, 0:NT], p_.h[:, 0:NT], mod(2, fc, b), xT.h[:, fc, 0:NT], ALU.mult, ALU.add,
                    [A(p_), A(modT), A(xT, allt(fc))], [A(xT, allt(fc))])
                sq_t = role("n_sq", [0, 1])
                actf(bfv(sq_t, NT), xT.h[:, fc, 0:NT], AF.Square, [A(xT, allt(fc))], [A(sq_t)])
                pend_sq.append((fc, sq_t))
            wfree(wt)
        while pend_sq:
            ones_mm(*pend_sq.pop(0))

        chk('O')
        chk('N1')
        rstd = role("n_rstd", [2])
        actf(rstd.h[:, 0:NT], pn.h[:, 0:NT], AF.Sqrt, [A(pn)], [A(rstd)], scale=1.0 / D, bias=EPS)
        kb.op("dve", dve.reciprocal, [A(rstd)], [A(rstd)], out=rstd.h[:, 0:NT], in_=rstd.h[:, 0:NT])
        chk('N2')
        for fc in range(16):
            if fc == 1:
                chk('N3')
            tmp = role("n_tmp", [3, 4])
            tt("pool" if fc % 2 else "dve", tmp.h[:, 0:NT], xT.h[:, fc, 0:NT], rstd.h[:, 0:NT], ALU.mult,
               [A(xT, allt(fc)), A(rstd)], [A(tmp)])
            actf(hT.h[:, fc, 0:NT], tmp.h[:, 0:NT], AF.Identity, [A(tmp), A(modT)], [A(hT, allt(fc))],
                 scale=mod(4, fc, b), bias=mod(3, fc, b))

        ps_lim[0] = 8
        chk('N')
        uT = mixT
        for g in range(4):
            hook('UD', g)
            for cc in range(8):
                if g in direct_groups:
                    wt = wload(None, 0, g * 2048 + cc * 256, q="pool", src_ap=w_up)
                else:
                    wt = wload(w_up_bf, 0, g * 2048 + cc * 256)
                for f in range(2):
                    j = cc * 2 + f
                    p_ = nps()
                    for kc in range(16):
                        mm(p_.h[:, 0:NT], wt.h[:, kc, f * 128:(f + 1) * 128], hT.h[:, kc, 0:NT], kc == 0, kc == 15,
                           [A(hT, allt(kc)), A(wt)], [A(p_)])
                    rl = role("u_rl", [5, 6, 7])
                    actf(rl.h[:, 0:NT], p_.h[:, 0:NT], AF.Relu, [A(p_)], [A(rl)])
                    tt("dve", uT.h[:, j, 0:NT], p_.h[:, 0:NT], rl.h[:, 0:NT], ALU.mult, [A(p_), A(rl)],
                       [A(uT, allt(j))])
                wfree(wt)
            for cc in range(8):
                if g in direct_groups:
                    wt = wload(None, g * 2048, cc * 256, q="pool", src_ap=w_down)
                else:
                    wt = wload(w_down_bf, g * 2048, cc * 256)
                for f in range(2):
                    fc = cc * 2 + f
                    p_ = nps()
                    for kc in range(16):
                        mm(p_.h[:, 0:NT], wt.h[:, kc, f * 128:(f + 1) * 128], uT.h[:, kc, 0:NT], kc == 0, kc == 15,
                           [A(uT, allt(kc)), A(wt)], [A(p_)])
                    stt(xT.h[:, fc, 0:NT], p_.h[:, 0:NT], mod(5, fc, b), xT.h[:, fc, 0:NT], ALU.mult, ALU.add,
                        [A(p_), A(modT), A(xT, allt(fc))], [A(xT, allt(fc))])
                wfree(wt)

        chk('UD')
        chk('Ystart')
        for t in range(ntile):
            for g in range(4):
                p_ = nps()
                for j in range(4):
                    fc = g * 4 + j
                    tr(p_.h[0:R, j * 128:(j + 1) * 128], xT.h[:, fc, t * 128:t * 128 + R], ident.h[:],
                       [A(xT, fc * 4 + t), A(ident)], [A(p_)], mark=(j == 3))
                yt = role("y_st", [0, 1, 2, 3, 4, 8, 9, 10])
                cp("act" if g % 2 == 0 else "dve", yt.h[0:R, 0:512], p_.h[0:R, :], [A(p_)], [A(yt)])
                kb.dma("act", y_dst[t * 128:t * 128 + R, g * 512:(g + 1) * 512], yt.h[0:R, 0:512], [A(yt)], [],
                       is_output=True)

    for s_ in range(NPS):
        kb.op("dve", dve.memset, [], [A(hcar)], ap=hcar.h[:], constant=0.0)
        kb.op("dve", dve.memset, [], [A(ccar)], ap=ccar.h[:], constant=0.0)
        for blk in range(T // BLK):
            tok0 = blk * BLK
            hk = None
            cap = 8
            if s_ == 0 and blk == 0:
                hk = first_hooks
                cap = 7
            if s_ == 0 and blk == 1:
                hk = second_hooks
            if s_ == NPS - 1 and blk == T // BLK - 1:
                hk = {("UD", 1): [load_sample_cache]}
            run_block(s_, "p", xp[s_, tok0:tok0 + BLK, :], yp[s_, tok0:tok0 + BLK, :],
                      kp_o[s_, tok0:tok0 + BLK, :], vp_o[s_, tok0:tok0 + BLK, :], tok0, BLK, blk * 4,
                      blk == 0, blk == T // BLK - 1, hp_o[s_], cp_o[s_], hooks=hk, ps_cap=cap,
                      direct_groups=((2, 3) if (s_ == 0 and blk == 0) else ()))

    chk('prompts')
    load(hcar, h0_d)
    load(ccar, conv0_d)
    run_block(2, "s", xsm, ysm, ks_o, vs_o, 0, TS, 16, True, True, hs_o, cs_o)

    kb.finish()
    return nc, kb


_CACHE = {}


def _rope_tables():
    half = 8
    inv_freq = (np.float32(500000.0) ** (-(np.arange(half, dtype=np.float32) * np.float32(2.0)) / np.float32(16))).astype(np.float32)
    pos = np.arange(NKT * 128, dtype=np.float32)
    ang = (pos[:, None] * inv_freq[None, :]).astype(np.float32)
    cos = np.cos(ang).astype(np.float32).reshape(NKT, 128, half).transpose(1, 0, 2)
    sin = np.sin(ang).astype(np.float32).reshape(NKT, 128, half).transpose(1, 0, 2)
    return np.ascontiguousarray(cos), np.ascontiguousarray(sin)


def kernel(x_prompt, x_sample, c_prompt, c_sample, cache_k, cache_v, state_lru_h, state_conv,
           w_ada, b_ada, w_in, g_q, g_k, lambda_q1, lambda_k1, lambda_q2, lambda_k2, g_subln,
           conv_w, conv_b, w_gate_a, b_gate_a, w_gate_x, b_gate_x, lru_lambda,
           w_out, w_up, w_down):
    f = lambda a: np.ascontiguousarray(np.asarray(a, dtype=np.float32))
    if "nc" not in _CACHE:
        _CACHE["nc"] = build()[0]
    nc = _CACHE["nc"]
    cos, sin = _rope_tables()
    fm = lambda v: f(np.asarray(v).reshape(-1, 128).T)
    shared = {
        "w_ada": f(w_ada[0]), "w_in": f(w_in[0]), "w_out": f(w_out[0]), "w_up": f(w_up[0]),
        "w_down": f(w_down[0]),
        "badaT": fm(b_ada[0]),
        "gq_b": f(np.tile(np.asarray(g_q[0]), (128, 4))),
        "gk_b": f(np.tile(np.asarray(g_k[0]), (128, 4))),
        "gsub_b": f(np.tile(np.asarray(g_subln[0]), (128, 1))),
        "lams_b": f(np.broadcast_to(np.stack([np.asarray(lambda_q1[0]), np.asarray(lambda_k1[0]),
                                               np.asarray(lambda_q2[0]), np.asarray(lambda_k2[0])])[None],
                                    (128, 4, 64))),
        "convw": f(np.asarray(conv_w[0]).reshape(4, 8, 128).transpose(2, 1, 0)),
        "convb": fm(conv_b[0]),
        "wga": f(w_gate_a[0]), "wgx": f(w_gate_x[0]),
        "bga": fm(np.asarray(b_gate_a[0]).reshape(-1)), "bgx": fm(np.asarray(b_gate_x[0]).reshape(-1)),
        "lam_lru": fm(lru_lambda[0]),
        "cos_t": cos, "sin_t": sin,
        "ident": np.eye(128, dtype=np.float32),
    }
    xpn = np.asarray(x_prompt)
    xsn = np.asarray(x_sample)
    in_maps = []
    for i in range(8):
        c3 = np.stack([np.asarray(c_prompt[2 * i]), np.asarray(c_prompt[2 * i + 1]), np.asarray(c_sample[i])])
        m = dict(shared)
        m["xp"] = f(xpn[2 * i:2 * i + 2])
        m["xs"] = f(xsn[i])
        m["cT"] = f(c3.reshape(3, 16, 128).transpose(2, 1, 0))
        m["ck"] = f(np.asarray(cache_k[0, i]).reshape(PAST, 1024))
        m["cv"] = f(np.asarray(cache_v[0, i]).reshape(PAST, 1024))
        m["h0"] = fm(state_lru_h[0, i])
        m["conv0"] = f(np.asarray(state_conv[0, i]).reshape(3, 8, 128).transpose(2, 1, 0))
        in_maps.append(m)
    res = run_bass_kernel_spmd(nc, in_maps, core_ids=list(range(8)))
    R = res.results
    y_p = np.concatenate([r["yp"] for r in R], axis=0)
    y_s = np.stack([r["ys"] for r in R], axis=0)
    k_p = np.concatenate([r["kp"] for r in R], axis=0).reshape(1, 16, T, NH, 2, 64)
    v_p = np.concatenate([r["vp"] for r in R], axis=0).reshape(1, 16, T, NH, 128)
    h_p = np.concatenate([r["hp"].transpose(0, 2, 1).reshape(NPS, 1024) for r in R], axis=0)[None]
    c_p = np.concatenate([r["cp"].transpose(0, 3, 2, 1).reshape(NPS, 3, 1024) for r in R], axis=0)[None]
    k_s = np.stack([r["ksn"] for r in R], axis=0).reshape(1, 8, TS, NH, 2, 64)
    v_s = np.stack([r["vsn"] for r in R], axis=0).reshape(1, 8, TS, NH, 128)
    h_s = np.stack([r["hsn"].T.reshape(1024) for r in R], axis=0)[None]
    c_s = np.stack([r["csn"].transpose(2, 1, 0).reshape(3, 1024) for r in R], axis=0)[None]
    outs = (y_p, y_s, k_p, v_p, h_p, c_p, k_s, v_s, h_s, c_s)
    return tuple(np.ascontiguousarray(o, dtype=np.float32) for o in outs)
```
